# Optimizing a Trainium2 kernel written in Bass

```python
import math
import numpy as np
import jax
import jax.numpy as jnp
from jax import lax

D_MODEL = 1024
BATCH = 2
SEQ = 8192
DEPTH = 2

HEAD_DIM = 64
N_HEADS = 8
MIX_W = N_HEADS * HEAD_DIM
N_BRANCH = 4
RMS_EPS = 1e-6
CONV_K = 4
MOBA_BLOCK = 256
MOBA_TOPK = 3
MOBA_QBLOCK = 128
REL_BUCKETS = 32
REL_MAX_DIST = 128
GDN_CHUNK = 64
RWKV_W_RANK = 64
RWKV_A_RANK = 64
RWKV_G_RANK = 128
RWKV_V_RANK = 32
RWKV_LN_EPS = 64e-5
RWKV_COLS = 3 * MIX_W + RWKV_W_RANK + RWKV_A_RANK + RWKV_G_RANK
SSM_GROUPS = 2
SSM_STATE = 128
SSM_BC = SSM_GROUPS * SSM_STATE
SSD_CHUNK = 128
D_FF = ((8 * D_MODEL + 3 * 256 - 1) // (3 * 256)) * 256
IN_WIDTHS = (
    3 * MIX_W,
    3 * MIX_W,
    MIX_W,
    N_HEADS,
    N_HEADS,
    RWKV_COLS,
    MIX_W,
    MIX_W + 2 * SSM_BC,
    N_HEADS,
    N_BRANCH * D_MODEL,
)
N_IN = sum(IN_WIDTHS)

kernel_name = 'hybrid_moba_gdn_rwkv7_mamba2_block'


def split_cols(t, widths):
    idx = np.cumsum(widths)[:-1].tolist()
    return jnp.split(t, idx, axis=-1)


def rmsnorm(x, w):
    xf = x.astype(jnp.float32)
    y = xf * lax.rsqrt(jnp.mean(xf * xf, axis=-1, keepdims=True) + RMS_EPS)
    return (y * w.astype(jnp.float32)).astype(x.dtype)


def l2norm(x):
    xf = x.astype(jnp.float32)
    return xf * lax.rsqrt(jnp.sum(xf * xf, axis=-1, keepdims=True) + 1e-6)


def causal_dwconv(x, w, b=None):
    y = lax.conv_general_dilated(
        x, w[:, None, :].astype(x.dtype), window_strides=(1,),
        padding=[(w.shape[0] - 1, 0)], dimension_numbers=('NWC', 'WIO', 'NWC'),
        feature_group_count=x.shape[-1])
    if b is not None:
        y = y + b
    return y


def token_shift(x):
    return jnp.pad(x, ((0, 0), (1, 0), (0, 0)))[:, :-1]


def t5_bucket(dist):
    n = jnp.maximum(dist, 0)
    max_exact = REL_BUCKETS // 2
    nf = jnp.maximum(n, max_exact).astype(jnp.float32)
    large = max_exact + (jnp.log(nf / max_exact) / math.log(REL_MAX_DIST / max_exact)
                         * (REL_BUCKETS - max_exact)).astype(jnp.int32)
    large = jnp.minimum(large, REL_BUCKETS - 1)
    return jnp.where(n < max_exact, n, large)


def moba_attention(q, k, v, rel_bias):
    Bsz, S, H, Dh = q.shape
    nb = -(-S // MOBA_BLOCK)
    pad = nb * MOBA_BLOCK - S
    scale = Dh ** -0.5
    qh = q.transpose(0, 2, 1, 3)
    padk = ((0, 0), (0, pad), (0, 0), (0, 0))
    kp = jnp.pad(k, padk).transpose(0, 2, 1, 3).reshape(Bsz, H, nb, MOBA_BLOCK, Dh)
    vp = jnp.pad(v, padk).transpose(0, 2, 1, 3).reshape(Bsz, H, nb, MOBA_BLOCK, Dh)
    kmean = kp.mean(axis=3)
    q_blk = jnp.arange(S) // MOBA_BLOCK
    past = jnp.arange(nb)[None, :] < q_blk[:, None]
    gate = jnp.einsum('bhsd,bhnd->bhsn', qh, kmean).astype(jnp.float32)
    gate = jnp.where(past, gate, -jnp.inf)
    k_sel = min(MOBA_TOPK, nb)
    _, sel = lax.top_k(gate, k_sel)
    sel_valid = sel < q_blk[:, None]
    bi = jnp.arange(Bsz)[:, None, None, None]
    hi = jnp.arange(H)[None, :, None, None]
    offs = jnp.arange(MOBA_BLOCK)
    bias_t = rel_bias.T

    def one_chunk(c):
        q0 = c * MOBA_QBLOCK
        qpos = q0 + jnp.arange(MOBA_QBLOCK)
        qc = lax.dynamic_slice_in_dim(qh, q0, MOBA_QBLOCK, axis=2)
        sc = lax.dynamic_slice_in_dim(sel, q0, MOBA_QBLOCK, axis=2)
        vc = lax.dynamic_slice_in_dim(sel_valid, q0, MOBA_QBLOCK, axis=2)
        kg = kp[bi, hi, sc]
        vg = vp[bi, hi, sc]
        s_sel = jnp.einsum('bhqd,bhqjkd->bhqjk', qc, kg).astype(jnp.float32) * scale
        kpos_sel = sc[..., None] * MOBA_BLOCK + offs
        s_sel = s_sel + bias_t[hi[..., None], t5_bucket(qpos[:, None, None] - kpos_sel)]
        s_sel = jnp.where(vc[..., None], s_sel, -jnp.inf)
        own = q0 // MOBA_BLOCK
        ko = lax.dynamic_slice_in_dim(kp, own, 1, axis=2)[:, :, 0]
        vo = lax.dynamic_slice_in_dim(vp, own, 1, axis=2)[:, :, 0]
        d_own = qpos[:, None] - (own * MOBA_BLOCK + offs)[None, :]
        s_own = jnp.einsum('bhqd,bhkd->bhqk', qc, ko).astype(jnp.float32) * scale
        s_own = jnp.where(d_own >= 0, s_own + bias_t[:, t5_bucket(d_own)], -jnp.inf)
        n_sel = k_sel * MOBA_BLOCK
        s = jnp.concatenate([s_sel.reshape(Bsz, H, MOBA_QBLOCK, n_sel), s_own], axis=-1)
        p = jax.nn.softmax(s, axis=-1).astype(v.dtype)
        p_sel = p[..., :n_sel].reshape(Bsz, H, MOBA_QBLOCK, k_sel, MOBA_BLOCK)
        return (jnp.einsum('bhqjk,bhqjkd->bhqd', p_sel, vg)
                + jnp.einsum('bhqk,bhkd->bhqd', p[..., n_sel:], vo))

    o = lax.map(one_chunk, jnp.arange(S // MOBA_QBLOCK))
    return o.transpose(1, 0, 3, 2, 4).reshape(Bsz, S, H * Dh)


def gated_deltanet(q, k, v, beta, g):
    Bsz, S, H, Dk = q.shape
    Dv = v.shape[-1]
    C = GDN_CHUNK
    N = S // C
    ch = lambda t: t.reshape(Bsz, N, C, H, t.shape[-1]).transpose(0, 3, 1, 2, 4)
    q, k, v = ch(q), ch(k), ch(v)
    beta = beta.reshape(Bsz, N, C, H).transpose(0, 3, 1, 2)
    gc = jnp.cumsum(g.reshape(Bsz, N, C, H).transpose(0, 3, 1, 2), axis=-1)
    tri_incl = jnp.tril(jnp.ones((C, C), bool))
    tri_strict = jnp.tril(jnp.ones((C, C), bool), -1)
    decay = jnp.exp(jnp.where(tri_incl, gc[..., :, None] - gc[..., None, :], -jnp.inf))
    kb = k * beta[..., None]
    Lm = jnp.where(tri_strict, jnp.einsum('bhnid,bhnjd->bhnij', kb, k) * decay, 0.0)
    eye = jnp.eye(C, dtype=jnp.float32)
    T = lax.linalg.triangular_solve(Lm + eye, jnp.broadcast_to(eye, Lm.shape),
                                    left_side=True, lower=True, unit_diagonal=True)
    u = T @ (v * beta[..., None])
    w = T @ (kb * jnp.exp(gc)[..., None])
    a_in = jnp.where(tri_incl, jnp.einsum('bhnid,bhnjd->bhnij', q, k) * decay, 0.0)
    q_dec = q * jnp.exp(gc)[..., None]
    k_dec = k * jnp.exp(gc[..., -1:] - gc)[..., None]
    g_last = jnp.exp(gc[..., -1])

    def step(st, xs):
        u_c, w_c, qd_c, kd_c, a_c, gl_c = xs
        v_new = u_c - w_c @ st
        o_c = qd_c @ st + a_c @ v_new
        st = st * gl_c[..., None, None] + jnp.swapaxes(kd_c, -1, -2) @ v_new
        return st, o_c

    xs = tuple(jnp.moveaxis(t, 2, 0) for t in (u, w, q_dec, k_dec, a_in, g_last))
    _, o = lax.scan(step, jnp.zeros((Bsz, H, Dk, Dv), jnp.float32), xs)
    return o.transpose(1, 0, 3, 2, 4).reshape(Bsz, S, H, Dv)


def rwkv7_wkv(r, w, k, v, kk, a):
    Bsz, S, H, N = r.shape

    def step(st, xs):
        r_t, w_t, k_t, v_t, kk_t, a_t = xs
        sa = jnp.einsum('bhvk,bhk->bhv', st, -kk_t)
        st = (st * w_t[:, :, None, :] + sa[..., :, None] * (kk_t * a_t)[..., None, :]
              + v_t[..., :, None] * k_t[..., None, :])
        return st, jnp.einsum('bhvk,bhk->bhv', st, r_t)

    xs = tuple(t.transpose(1, 0, 2, 3) for t in (r, w, k, v, kk, a))
    _, y = lax.scan(step, jnp.zeros((Bsz, H, N, N), jnp.float32), xs)
    return y.transpose(1, 0, 2, 3)


def ssd_chunked(x, dt, A, Bm, Cm):
    Bsz, S, H, P = x.shape
    G, N = Bm.shape[2], Bm.shape[3]
    J = H // G
    L = SSD_CHUNK
    NC = S // L
    xg = (x * dt[..., None]).reshape(Bsz, NC, L, G, J, P)
    acs = jnp.cumsum((dt * A).reshape(Bsz, NC, L, G, J), axis=2)
    Bc = Bm.reshape(Bsz, NC, L, G, N)
    Cc = Cm.reshape(Bsz, NC, L, G, N)
    tri = jnp.tril(jnp.ones((L, L), bool))[:, :, None, None]
    seg = acs[:, :, :, None] - acs[:, :, None, :]
    Lmat = jnp.exp(jnp.where(tri, seg, -jnp.inf))
    cb = jnp.einsum('bctgn,bcsgn->bctsg', Cc, Bc)
    y_diag = jnp.einsum('bctsg,bctsgj,bcsgjp->bctgjp', cb, Lmat, xg)
    decay_states = jnp.exp(acs[:, :, -1:] - acs)
    states = jnp.einsum('bcsgn,bcsgj,bcsgjp->bcgjpn', Bc, decay_states, xg)
    chunk_decay = jnp.exp(acs[:, :, -1])

    def step(h, xs):
        st, cd = xs
        return h * cd[..., None, None] + st, h

    _, h_in = lax.scan(step, jnp.zeros((Bsz, G, J, P, N), jnp.float32),
                       (states.transpose(1, 0, 2, 3, 4, 5), chunk_decay.transpose(1, 0, 2, 3)))
    h_in = h_in.transpose(1, 0, 2, 3, 4, 5)
    y_off = jnp.einsum('bctgn,bcgjpn,bctgj->bctgjp', Cc, h_in, jnp.exp(acs))
    return (y_diag + y_off).reshape(Bsz, S, H, P)


def setup_inputs(seed: int = 0) -> dict:
    key = jax.random.key(seed)
    keys = jax.random.split(key, 40)
    counter = [0]
    L, f32 = DEPTH, jnp.float32

    def nxt():
        counter[0] += 1
        return keys[counter[0] - 1]

    def nrm(shape, scale):
        return jax.random.normal(nxt(), shape, f32) * scale

    def uni(shape, lo, hi):
        return jax.random.uniform(nxt(), shape, f32, lo, hi)

    def dt_bias(shape):
        dt = jnp.exp(uni(shape, math.log(1e-3), math.log(1e-1)))
        return dt + jnp.log(-jnp.expm1(-dt))

    return {
        'x': nrm((BATCH, SEQ, D_MODEL), 1.0),
        'rel_bias': nrm((REL_BUCKETS, N_HEADS), 0.5),
        'norm1_w': 1.0 + nrm((L, D_MODEL), 0.02),
        'w_in': nrm((L, D_MODEL, N_IN), D_MODEL ** -0.5),
        'moba_q_norm': 1.0 + nrm((L, HEAD_DIM), 0.02),
        'moba_k_norm': 1.0 + nrm((L, HEAD_DIM), 0.02),
        'gdn_conv_w': nrm((L, CONV_K, 3 * MIX_W), CONV_K ** -0.5),
        'gdn_A_log': jnp.log(uni((L, N_HEADS), 1.0, 16.0)),
        'gdn_dt_bias': dt_bias((L, N_HEADS)),
        'gdn_norm_w': 1.0 + nrm((L, HEAD_DIM), 0.02),
        'rwkv_mu': uni((L, RWKV_COLS), 0.0, 1.0),
        'rwkv_w0': uni((L, MIX_W), -6.0, -1.0),
        'rwkv_w_up': nrm((L, RWKV_W_RANK, MIX_W), 0.5 * RWKV_W_RANK ** -0.5),
        'rwkv_a0': nrm((L, MIX_W), 0.1),
        'rwkv_a_up': nrm((L, RWKV_A_RANK, MIX_W), 0.5 * RWKV_A_RANK ** -0.5),
        'rwkv_g_up': nrm((L, RWKV_G_RANK, MIX_W), RWKV_G_RANK ** -0.5),
        'rwkv_k_k': 0.85 + nrm((L, MIX_W), 0.02),
        'rwkv_k_a': 1.0 + nrm((L, MIX_W), 0.02),
        'rwkv_r_k': nrm((L, N_HEADS, HEAD_DIM), 0.1),
        'rwkv_v0': 0.5 + nrm((L - 1, MIX_W), 0.1),
        'rwkv_v_down': nrm((L - 1, MIX_W, RWKV_V_RANK), MIX_W ** -0.5),
        'rwkv_v_up': nrm((L - 1, RWKV_V_RANK, MIX_W), 0.5 * RWKV_V_RANK ** -0.5),
        'rwkv_ln_w': 1.0 + nrm((L, MIX_W), 0.02),
        'rwkv_ln_b': nrm((L, MIX_W), 0.02),
        'mamba_conv_w': nrm((L, CONV_K, MIX_W + 2 * SSM_BC), CONV_K ** -0.5),
        'mamba_conv_b': nrm((L, MIX_W + 2 * SSM_BC), 0.02),
        'mamba_dt_bias': dt_bias((L, N_HEADS)),
        'mamba_A_log': jnp.log(uni((L, N_HEADS), 1.0, 16.0)),
        'mamba_D': 1.0 + nrm((L, N_HEADS), 0.1),
        'mamba_norm_w': 1.0 + nrm((L, MIX_W), 0.02),
        'w_branch': nrm((L, N_BRANCH, MIX_W, D_MODEL), MIX_W ** -0.5),
        'w_out': nrm((L, D_MODEL, D_MODEL), D_MODEL ** -0.5),
        'norm2_w': 1.0 + nrm((L, D_MODEL), 0.02),
        'ffn_w_in': nrm((L, D_MODEL, 2 * D_FF), D_MODEL ** -0.5),
        'ffn_w_down': nrm((L, D_FF, D_MODEL), D_FF ** -0.5),
    }


def reference(x, rel_bias, norm1_w, w_in, moba_q_norm, moba_k_norm, gdn_conv_w, gdn_A_log,
              gdn_dt_bias, gdn_norm_w, rwkv_mu, rwkv_w0, rwkv_w_up, rwkv_a0, rwkv_a_up,
              rwkv_g_up, rwkv_k_k, rwkv_k_a, rwkv_r_k, rwkv_v0, rwkv_v_down, rwkv_v_up,
              rwkv_ln_w, rwkv_ln_b, mamba_conv_w, mamba_conv_b, mamba_dt_bias, mamba_A_log,
              mamba_D, mamba_norm_w, w_branch, w_out, norm2_w, ffn_w_in, ffn_w_down):
    Bsz, S, _ = x.shape
    H, Dh = N_HEADS, HEAD_DIM
    f32 = jnp.float32
    heads = lambda t: t.reshape(Bsz, S, H, -1)
    v_first = None
    for i in range(DEPTH):
        h = rmsnorm(x, norm1_w[i])
        proj = h @ w_in[i]
        (a_qkv, b_qkv, b_z, b_beta, b_a, c_all, d_z, d_xbc, d_dt,
         gate_logits) = split_cols(proj, IN_WIDTHS)

        a_q, a_k, a_v = jnp.split(a_qkv, 3, axis=-1)
        y_a = moba_attention(rmsnorm(heads(a_q), moba_q_norm[i]),
                             rmsnorm(heads(a_k), moba_k_norm[i]), heads(a_v), rel_bias)

        qkv = jax.nn.silu(causal_dwconv(b_qkv.astype(f32), gdn_conv_w[i]))
        g_q, g_k, g_v = jnp.split(qkv, 3, axis=-1)
        beta = jax.nn.sigmoid(b_beta.astype(f32))
        g_log = -jnp.exp(gdn_A_log[i]) * jax.nn.softplus(b_a.astype(f32) + gdn_dt_bias[i])
        o_b = gated_deltanet(l2norm(heads(g_q)) * (Dh ** -0.5), l2norm(heads(g_k)),
                             heads(g_v), beta, g_log)
        y_b = (rmsnorm(o_b, gdn_norm_w[i]) * jax.nn.silu(heads(b_z.astype(f32)))).reshape(Bsz, S, MIX_W)

        c = c_all.astype(f32)
        c = c + (token_shift(c) - c) * rwkv_mu[i]
        c_r, c_k, c_v, c_wd, c_ad, c_gd = split_cols(
            c, (MIX_W, MIX_W, MIX_W, RWKV_W_RANK, RWKV_A_RANK, RWKV_G_RANK))
        w_log = -jax.nn.softplus(-(rwkv_w0[i] + jnp.tanh(c_wd) @ rwkv_w_up[i])) - 0.5
        decay = jnp.exp(-jnp.exp(w_log))
        a_in = jax.nn.sigmoid(rwkv_a0[i] + c_ad @ rwkv_a_up[i])
        g_out = jax.nn.sigmoid(c_gd) @ rwkv_g_up[i]
        if i == 0:
            v_first = c_v
            v_r = c_v
        else:
            lam = jax.nn.sigmoid(rwkv_v0[i - 1] + (c_v @ rwkv_v_down[i - 1]) @ rwkv_v_up[i - 1])
            v_r = c_v + (v_first - c_v) * lam
        kk = l2norm(heads(c_k * rwkv_k_k[i]))
        k_r = c_k * (1.0 + (a_in - 1.0) * rwkv_k_a[i])
        r_h, k_h, v_h = heads(c_r), heads(k_r), heads(v_r)
        wkv = rwkv7_wkv(r_h, heads(decay), k_h, v_h, kk, heads(a_in))
        mu = jnp.mean(wkv, axis=-1, keepdims=True)
        var = jnp.mean(jnp.square(wkv - mu), axis=-1, keepdims=True)
        y_c = ((wkv - mu) * lax.rsqrt(var + RWKV_LN_EPS)).reshape(Bsz, S, MIX_W) * rwkv_ln_w[i] + rwkv_ln_b[i]
        bonus = jnp.sum(r_h * k_h * rwkv_r_k[i], axis=-1, keepdims=True) * v_h
        y_c = (y_c + bonus.reshape(Bsz, S, MIX_W)) * g_out

        xbc = jax.nn.silu(causal_dwconv(d_xbc.astype(f32), mamba_conv_w[i], mamba_conv_b[i]))
        m_x, m_B, m_C = split_cols(xbc, (MIX_W, SSM_BC, SSM_BC))
        dt = jax.nn.softplus(d_dt.astype(f32) + mamba_dt_bias[i])
        m_xh = heads(m_x)
        y_d = ssd_chunked(m_xh, dt, -jnp.exp(mamba_A_log[i].astype(f32)),
                          m_B.reshape(Bsz, S, SSM_GROUPS, SSM_STATE),
                          m_C.reshape(Bsz, S, SSM_GROUPS, SSM_STATE))
        y_d = y_d + m_xh * mamba_D[i][:, None]
        yz = y_d.reshape(Bsz, S, MIX_W) * jax.nn.silu(d_z.astype(f32))
        y_d = rmsnorm(yz.reshape(Bsz, S, SSM_GROUPS, -1),
                      mamba_norm_w[i].reshape(SSM_GROUPS, -1)).reshape(Bsz, S, MIX_W)

        ys = jnp.stack([y_a.astype(x.dtype), y_b.astype(x.dtype),
                        y_c.astype(x.dtype), y_d.astype(x.dtype)], axis=2)
        branch = jnp.einsum('bsnc,ncd->bsnd', ys, w_branch[i])
        gates = jax.nn.sigmoid(gate_logits.reshape(Bsz, S, N_BRANCH, D_MODEL))
        x = x + jnp.sum(gates * branch, axis=2) @ w_out[i]

        h2 = rmsnorm(x, norm2_w[i])
        ff_g, ff_u = jnp.split(h2 @ ffn_w_in[i], 2, axis=-1)
        x = x + (jax.nn.silu(ff_g) * ff_u) @ ffn_w_down[i]
    return x
```

```python
import numpy as np
from contextlib import ExitStack
import concourse.bass as bass
import concourse.mybir as mybir
from concourse.bass_utils import run_bass_kernel_spmd

F32 = mybir.dt.float32
BF16 = mybir.dt.bfloat16
AF = mybir.ActivationFunctionType
ALU = mybir.AluOpType
AX = mybir.AxisListType

EPOCH = 30000
DMA_POOL = 16
DMA_MAXUSE = 180


class View:
    __slots__ = ("T", "ap")

    def __init__(self, T, ap):
        self.T = T
        self.ap = ap

    def __getitem__(self, idx):
        return View(self.T, self.ap[idx])

    def map(self, fn):
        return View(self.T, fn(self.ap))


class T:
    __slots__ = ("t", "w", "r", "kids", "parent", "name", "excl")

    def __init__(self, t, name="", parent=None):
        self.excl = False
        self.t = t
        self.w = None
        self.r = {}
        self.kids = []
        self.parent = parent
        self.name = name

    def sub(self, name=""):
        k = T(self.t, name, parent=self)
        self.kids.append(k)
        return k

    def __getitem__(self, idx):
        return View(self, self.t[idx])

    def _group(self):
        g = [self] + self.kids
        if self.parent is not None:
            g.append(self.parent)
        return g


def _ap(x):
    return x.ap if isinstance(x, View) else x


def _ts(*xs):
    return [x.T for x in xs if isinstance(x, View)]


class Sched:
    def __init__(self, nc, stack):
        self.nc = nc
        self.stack = stack
        self.streams = {"pe": [], "act": [], "dve": [], "pool": [], "sp": []}
        self.count = {"pe": 0, "act": 0, "dve": 0, "pool": 0}
        self.sems = {}
        self.known = {}
        self.dma_pool = {}
        self.dma_rr = {}
        self.nsem = 0
        self.final_tokens = []
        self.psn = 0
        self.psums = None

    def sb(self, name, shape, dt):
        self._uid = getattr(self, "_uid", 0) + 1
        name = f"{name}_u{self._uid}"
        t = self.stack.enter_context(self.nc.sbuf_tensor(name, list(shape), dt))
        return T(t, name)

    def ps(self, name, shape, dt=F32):
        t = self.stack.enter_context(self.nc.psum_tensor(name, list(shape), dt))
        r = T(t, name)
        r.excl = True
        return r

    def psum(self):
        if self.psums is None:
            self.psums = [self.ps(f"psb{i}", [128, 512], F32) for i in range(8)]
        p = self.psums[self.psn % 8]
        self.psn += 1
        return p

    def _newsem(self, name):
        self.nsem += 1
        return self.stack.enter_context(self.nc.semaphore(name))

    def _eng_token(self, eng):
        c = self.count[eng]
        ep, k = divmod(c, EPOCH)
        key = (eng, ep)
        if key not in self.sems:
            self.sems[key] = self._newsem(f"s_{eng}_{ep}")
        self.count[eng] = c + 1
        return (key, k + 1, eng, c)

    def _dma_token(self, q):
        pool = self.dma_pool.setdefault(q, [])
        rr = self.dma_rr.get(q, 0)
        if len(pool) < DMA_POOL:
            pool.append([("dma", q, len(pool), 0), 0, 0])
            slot = len(pool) - 1
        else:
            slot = rr % DMA_POOL
        self.dma_rr[q] = rr + 1
        ent = pool[slot]
        if ent[1] >= DMA_MAXUSE:
            ent[2] += 1
            ent[0] = ("dma", q, slot, ent[2])
            ent[1] = 0
        key = ent[0]
        if key not in self.sems:
            self.sems[key] = self._newsem(f"s_dma_{q}_{slot}_{ent[2]}")
        prev = (key, ent[1] * 16, None, None) if ent[1] > 0 else None
        ent[1] += 1
        return (key, ent[1] * 16, None, None), prev

    def _need(self, waiter, tok, waits):
        if tok is None:
            return
        key, val, cls, gidx = tok
        if cls is not None:
            if self.known.get((waiter, cls), -1) >= gidx:
                return
            if waiter == "pe" and cls == "pe":
                return
            cur = waits.get(("E", cls))
            if cur is None or cur[3] < gidx:
                waits[("E", cls)] = tok
        else:
            if self.known.get((waiter, key), -1) >= val:
                return
            cur = waits.get(key)
            if cur is None or cur[1] < val:
                waits[key] = tok

    def op(self, eng, fn, reads=(), writes=(), dma=False, final=False):
        ex = [r for r in reads if r.excl]
        if ex:
            writes = list(writes) + [r for r in ex if r not in writes]
        waits = {}
        pb = getattr(self, "pending_barrier", None)
        if pb and eng in pb:
            for tk in pb.pop(eng):
                self._need(eng, tk, waits)
        for r in reads:
            for g in r._group():
                self._need(eng, g.w, waits)
        for w in writes:
            for g in w._group():
                self._need(eng, g.w, waits)
                for rt in g.r.values():
                    self._need(eng, rt, waits)
        if dma:
            tok, prev = self._dma_token(eng)
            self._need(eng, prev, waits)
            inc = (tok[0], 16)
        else:
            tok = self._eng_token(eng)
            inc = (tok[0], 1)
        wl = []
        for t in waits.values():
            wl.append((t[0], t[1]))
            if t[2] is not None:
                self.known[(eng, t[2])] = max(self.known.get((eng, t[2]), -1), t[3])
            else:
                self.known[(eng, t[0])] = max(self.known.get((eng, t[0]), -1), t[1])
        self.streams[eng].append((wl, fn, inc))
        rk = tok[2] if tok[2] is not None else tok[0]
        for r in reads:
            r.r[rk] = tok
        for w in writes:
            w.w = tok
            w.r = {}
            for k in w.kids:
                k.w = tok
                k.r = {}
        if final:
            self.final_tokens.append(tok)
        y = getattr(getattr(self, "_tls", None), "yield_", None)
        if y is not None:
            y()
        return tok

    def mm(self, out, lhsT, rhs, start=True, stop=True):
        o, l, r = _ap(out), _ap(lhsT), _ap(rhs)
        self.op("pe", lambda e: e.matmul(o, lhsT=l, rhs=r, start=start, stop=stop),
                reads=_ts(lhsT, rhs), writes=_ts(out))

    def transpose(self, out, in_, ident):
        o, i, d = _ap(out), _ap(in_), _ap(ident)
        self.op("pe", lambda e: e.transpose(o, i, d), reads=_ts(in_, ident), writes=_ts(out))

    def act(self, out, in_, func, bias=None, scale=None, accum=None):
        o, i = _ap(out), _ap(in_)
        kw = {}
        if bias is not None:
            kw["bias"] = _ap(bias)
        if scale is not None:
            kw["scale"] = _ap(scale)
        if accum is not None:
            kw["accum_out"] = _ap(accum)
        self.op("act", lambda e: e.activation(out=o, in_=i, func=func, **kw),
                reads=_ts(in_, bias, scale), writes=_ts(out, accum))

    def ts(self, out, in0, s1, s2, op0, op1=None, eng="dve", accum=None):
        o, i = _ap(out), _ap(in0)
        a1, a2 = _ap(s1), _ap(s2)
        kw = {}
        if op1 is not None:
            kw["op1"] = op1
        if accum is not None:
            kw["accum_out"] = _ap(accum)
        self.op(eng, lambda e: e.tensor_scalar(out=o, in0=i, scalar1=a1, scalar2=a2, op0=op0, **kw),
                reads=_ts(in0, s1, s2), writes=_ts(out, accum))

    def tt(self, out, in0, in1, op, eng="dve"):
        o, a, b = _ap(out), _ap(in0), _ap(in1)
        self.op(eng, lambda e: e.tensor_tensor(out=o, in0=a, in1=b, op=op),
                reads=_ts(in0, in1), writes=_ts(out))

    def stt(self, out, in0, scalar, in1, op0, op1, accum=None):
        o, a, s, b = _ap(out), _ap(in0), _ap(scalar), _ap(in1)
        kw = {}
        if accum is not None:
            kw["accum_out"] = _ap(accum)
        self.op("dve", lambda e: e.scalar_tensor_tensor(out=o, in0=a, scalar=s, in1=b, op0=op0, op1=op1, **kw),
                reads=_ts(in0, scalar, in1), writes=_ts(out, accum))

    def scan(self, out, d0, d1, init, op0, op1):
        o, a, b, c = _ap(out), _ap(d0), _ap(d1), _ap(init)
        self.op("dve", lambda e: e.tensor_tensor_scan(out=o, data0=a, data1=b, initial=c, op0=op0, op1=op1),
                reads=_ts(d0, d1, init), writes=_ts(out))

    def copy(self, out, in_, eng="dve"):
        o, i = _ap(out), _ap(in_)
        if eng == "act":
            self.op("act", lambda e: e.copy(out=o, in_=i), reads=_ts(in_), writes=_ts(out))
        else:
            self.op(eng, lambda e: e.tensor_copy(out=o, in_=i), reads=_ts(in_), writes=_ts(out))

    def memset(self, out, val, eng="pool"):
        o = _ap(out)
        self.op(eng, lambda e: e.memset(o, val), writes=_ts(out))

    def reduce(self, out, in_, op, axis=AX.X, eng="dve"):
        o, i = _ap(out), _ap(in_)
        self.op(eng, lambda e: e.tensor_reduce(out=o, in_=i, axis=axis, op=op), reads=_ts(in_), writes=_ts(out))

    def recip(self, out, in_):
        o, i = _ap(out), _ap(in_)
        self.op("dve", lambda e: e.reciprocal(out=o, in_=i), reads=_ts(in_), writes=_ts(out))

    def dma(self, out, in_, q="sp", final=False):
        o, i = _ap(out), _ap(in_)
        self.op(q, lambda e: e.dma_start(out=o, in_=i), reads=_ts(in_), writes=_ts(out), dma=True, final=final)

    def cc(self, kind, alu, groups, in_ap, out_ap, reads=(), writes=()):
        key = ("cc", self.nsem)
        sem = self._newsem(f"s_cc_{self.nsem}")
        self.sems[key] = sem
        if not hasattr(self, "_ccscr"):
            self._ccscr = self.sb("ccscr", [128, 8], F32)
        scr = self._ccscr

        def fn(e):
            e.collective_compute(kind, alu, replica_groups=groups, ins=[in_ap], outs=[out_ap]).then_inc(sem, 1)
            e.wait_ge(sem, 1)
            return e.memset(scr.t[:, :], 0.0)
        return self.op("pool", fn, reads=list(reads), writes=list(writes) + [scr])

    def barrier(self):
        toks = []
        for e in ("act", "dve", "pool"):
            scr = self._bscr[e]
            if e == "act":
                toks.append(self.op(e, (lambda ap: (lambda en: en.memzero(ap)))(scr.t[:, :]), writes=[scr]))
            else:
                toks.append(self.op(e, (lambda ap: (lambda en: en.memset(ap, 0.0)))(scr.t[:, :]), writes=[scr]))
        bps = _psum6(self)
        toks.append(self.op("pe", (lambda o, a: (lambda en: en.matmul(o, lhsT=a, rhs=a, start=True, stop=True)))(bps.t[0:8, 0:8], self._bone.t[:, 0:8]),
                            reads=[self._bone], writes=[bps]))
        for q, pool in self.dma_pool.items():
            for ent in pool:
                if ent[1] > 0:
                    toks.append((ent[0], ent[1] * 16, None, None))
        self.pending_barrier = {e: list(toks) for e in ("pe", "act", "dve", "pool", "sp")}

    def scope(self):
        outer = self

        class _Scope:
            def __enter__(self_):
                self_.saved = outer.stack
                self_.es = ExitStack()
                self_.es.__enter__()
                outer.stack = self_.es
                outer._scope_extra = {}
                return self_

            def __exit__(self_, *a):
                outer.barrier()
                outer.stack = self_.saved
                for attr in ("_wstage",):
                    if hasattr(outer, attr) and getattr(outer, attr + "_scope", None) is self_:
                        delattr(outer, attr)
                self_.es.__exit__(None, None, None)
                return False
        return _Scope()

    def emit(self):
        nc = self.nc
        fw = {}
        for tok in self.final_tokens:
            self._need("sp", tok, fw)
        final_waits = [(t[0], t[1]) for t in fw.values()]
        sems = self.sems
        streams = self.streams
        with nc.Block() as block:
            def mk(engname):
                def body(e):
                    for wl, fn, inc in streams[engname]:
                        for key, val in wl:
                            e.wait_ge(sems[key], val)
                        ins = fn(e)
                        ins.then_inc(sems[inc[0]], inc[1])
                    if engname == "sp":
                        for key, val in final_waits:
                            e.wait_ge(sems[key], val)
                return body
            block.tensor(mk("pe"))
            block.scalar(mk("act"))
            block.vector(mk("dve"))
            block.gpsimd(mk("pool"))
            block.sync(mk("sp"))


def wlayout(W, kc=None):
    K, N = W.shape
    kc = K // 128
    no = (N + 127) // 128
    if no * 128 != N:
        W = np.concatenate([W, np.zeros((K, no * 128 - N), W.dtype)], axis=1)
    return np.ascontiguousarray(W.reshape(kc, 128, no, 128).transpose(2, 1, 0, 3).reshape(no, 128, kc * 128))


def load_weight_tile(S, wst, wbf, W_dram_tile, ncols, slot, cast_eng="pool"):
    S.dma(wst[slot][:, 0:ncols], W_dram_tile)
    S.copy(wbf[slot][:, 0:ncols], wst[slot][:, 0:ncols], eng=cast_eng)


def rmsnorm_fm(S, xch, hch, nw, ones_bf, sq, rstd, NT, epsb, D=1024):
    kc = len(xch)
    for t in range(NT // 512):
        sl = slice(t * 512, (t + 1) * 512)
        ps = S.psum()
        for c in range(kc):
            s = sq[c % 2]
            S.act(s[:, :], xch[c][:, sl], AF.Square)
            S.mm(ps[:, :], ones_bf[:, :], s[:, :], start=(c == 0), stop=(c == kc - 1))
        S.act(rstd[:, :], ps[:, :], AF.Sqrt, bias=epsb[:, 0:1], scale=1.0 / D)
        S.recip(rstd[:, :], rstd[:, :])
        for c in range(kc):
            S.stt(hch[c][:, sl], xch[c][:, sl], nw[:, c:c + 1], rstd[:, :], ALU.mult, ALU.mult)


def _psum_setup(S):
    S.psums = [S.ps(f"psb{i}", [128, 512], F32) for i in range(6)]
    S.accs = [S.ps(f"psacc{i}", [128, 512], F32) for i in range(2)]
    S.accn = 0
    S._bscr = {e: S.sb(f"bscr_{e}", [128, 1], F32) for e in ("act", "dve", "pool")}
    S._bone = S.sb("bscr_one", [128, 8], BF16)
    S.memset(S._bone[:, :], 0.0)


import threading


def _psum6(S):
    slot = getattr(S._tls, "slot", None) if hasattr(S, "_tls") else None
    if slot is None:
        p = S.psums[S.psn % 6]
        S.psn += 1
        return p
    n = S._slotn.get(slot, 0)
    S._slotn[slot] = n + 1
    return S.psums[2 * slot + (n % 2)]


def run_interleaved(S, tasks, width=3):
    if not hasattr(S, "_tls"):
        S._tls = threading.local()
        S._slotn = {}
    cond = threading.Condition()
    state = {"turn": None, "active": [], "pending": list(enumerate(tasks)), "err": None, "done": 0}
    free_slots = list(range(width))

    def yield_():
        me = threading.get_ident()
        with cond:
            act = state["active"]
            if len(act) > 1:
                i = act.index(me)
                state["turn"] = act[(i + 1) % len(act)]
                cond.notify_all()
                while state["turn"] != me and state["err"] is None:
                    cond.wait()

    def runner(fn, slot):
        me = threading.get_ident()
        with cond:
            state["active"].append(me)
            if state["turn"] is None:
                state["turn"] = me
            cond.notify_all()
            while state["turn"] != me and state["err"] is None:
                cond.wait()
        S._tls.slot = slot
        S._tls.yield_ = yield_
        try:
            fn()
        except BaseException as e:
            with cond:
                state["err"] = e
                cond.notify_all()
        with cond:
            act = state["active"]
            i = act.index(me)
            act.remove(me)
            free_slots.append(slot)
            state["done"] += 1
            if state["pending"] and state["err"] is None:
                _, nfn = state["pending"].pop(0)
                ns = free_slots.pop(0)
                th = threading.Thread(target=runner, args=(nfn, ns))
                state["turn"] = None
                th.start()
                while state["turn"] is None and state["err"] is None:
                    cond.wait()
            else:
                state["turn"] = act[i % len(act)] if act else None
            cond.notify_all()

    with cond:
        first = []
        while state["pending"] and len(first) < width:
            _, fn = state["pending"].pop(0)
            first.append((fn, free_slots.pop(0)))
    ths = []
    for fn, slot in first:
        th = threading.Thread(target=runner, args=(fn, slot))
        th.start()
        ths.append(th)
        with cond:
            while len(state["active"]) + state["done"] < len(ths) and state["err"] is None:
                cond.wait(timeout=0.01)
    with cond:
        while state["done"] < len(tasks) and state["err"] is None:
            cond.wait(timeout=0.05)
    if state["err"] is not None:
        raise state["err"]


def _acc(S):
    p = S.accs[S.accn % 2]
    S.accn += 1
    return p


def load_w_bf(S, name, dram_ap, ncols, kc=8):
    wt = S.sb(name, [128, kc, ncols], BF16)
    step = max(1, 2048 // kc)
    if getattr(S, "_wstage_stack", None) is not S.stack:
        S._wsid = getattr(S, "_wsid", 0) + 1
        S._wstage = [S.sb(f"wstage{S._wsid}_{i}", [128, 2048], F32) for i in range(2)]
        S._wstage_stack = S.stack
        S._wsn = 0
    for c0 in range(0, ncols, step):
        n = min(step, ncols - c0)
        stg = S._wstage[S._wsn % 2]
        S._wsn += 1
        sv = stg[:, 0:kc * n].map(lambda a: a.rearrange("p (c n) -> p c n", c=kc))
        S.dma(sv, dram_ap[:, :, c0:c0 + n])
        S.copy(wt[:, :, c0:c0 + n], sv, eng="pool")
    return wt


def proj_fm(S, ps, Wt, col0, M, hb, n=512, kc=8):
    for c in range(kc):
        S.mm(ps, Wt[:, c, col0:col0 + M], hb[:, c, 0:n], start=(c == 0), stop=(c == kc - 1))


def proj_tm(S, ps, Wt, col0, ncols, hb, sub, kc=8):
    for c in range(kc):
        S.mm(ps, hb[:, c, sub * 128:(sub + 1) * 128], Wt[:, c, col0:col0 + ncols], start=(c == 0), stop=(c == kc - 1))


def t5_bucket_np(n):
    n = np.maximum(n, 0)
    nf = np.maximum(n, 16).astype(np.float32)
    large = 16 + (np.log(nf / np.float32(16)) / np.float32(np.log(128 / 16)) * np.float32(16)).astype(np.int32)
    large = np.minimum(large, 31)
    return np.where(n < 16, n, large)


def moba_host_consts(rel_bias, h):
    ko = np.arange(128)[:, None]
    qo = np.arange(512)[None, :]
    tiles = []
    for delta in (128, 0, -128, -256, -384):
        d = delta + qo - ko
        bt = rel_bias[t5_bucket_np(d), h].astype(np.float32)
        bt = np.where(d >= 0, bt, np.float32(-30000.0))
        tiles.append(bt)
    biasT = np.stack(tiles, 0)
    b31 = np.full((128, 1), rel_bias[31, h], np.float32)
    return biasT, b31


def moba_onehot():
    oh = np.zeros((32, 8192), np.float32)
    oh[np.arange(8192) // 256, np.arange(8192)] = 1.0
    return oh


def moba_setup(S, D):
    R = {}
    R["Wt"] = load_w_bf(S, "moba_w", D["wa"], 192)
    R["Qaug"] = S.sb("Qaug", [96, 8192], BF16)
    R["Kaug"] = S.sb("Kaug", [96, 8192], BF16)
    R["Vaug"] = S.sb("Vaug", [128, 64, 65], BF16)
    R["kmean"] = S.sb("kmean", [64, 32], F32)
    R["qn32"] = S.sb("qn32", [64, 512], F32)
    R["kn32"] = S.sb("kn32", [64, 512], F32)
    R["sqb"] = S.sb("msqb", [64, 512], BF16)
    R["rs"] = S.sb("mrs", [64, 512], F32)
    R["km2"] = S.sb("km2", [64, 2], F32)
    R["gsb"] = [S.sb(f"gsb{i}", [128, 32], F32) for i in range(2)]
    R["mask"] = [S.sb(f"mmask{i}", [128, 96], F32) for i in range(2)]
    R["top8"] = S.sb("top8", [128, 8], F32)
    R["pt"] = [S.sb(f"mpt{i}", [128, 512], BF16) for i in range(3)]
    R["rden"] = S.sb("rden", [128, 512], F32)
    R["osb"] = S.sb("mosb", [64, 512], F32)
    R["ya"] = [S.sb(f"mya{i}", [64, 512], BF16) for i in range(2)]
    R["qw"] = S.sb("mqw", [64, 1], F32)
    R["kw"] = S.sb("mkw", [64, 1], F32)
    R["b31"] = S.sb("mb31", [128, 1], F32)
    R["zcol"] = S.sb("mzcol", [128, 1], F32)
    R["eps"] = S.sb("meps", [128, 1], F32)
    R["ones_f"] = S.sb("mones_f", [128, 128], F32)
    R["ones_bf"] = S.sb("mones_bf", [128, 128], BF16)
    R["ident"] = S.sb("mident", [128, 128], F32)
    R["ident_bf"] = S.sb("mident_bf", [128, 128], BF16)
    R["bhi"] = S.sb("mbhi", [128, 5, 512], BF16)
    R["blo"] = S.sb("mblo", [128, 5, 512], BF16)
    S.memset(R["zcol"][:, :], 0.0)
    S.memset(R["eps"][:, :], 1e-6)
    S.memset(R["ones_f"][:, :], 1.0)
    S.copy(R["ones_bf"][:, :], R["ones_f"][:, :], eng="pool")
    S.dma(R["ident"][:, :], D["ident"][:, :])
    S.copy(R["ident_bf"][:, :], R["ident"][:, :], eng="pool")
    S.dma(R["qw"][:, :], D["qw"][:, :])
    S.dma(R["kw"][:, :], D["kw"][:, :])
    S.ts(R["qw"][:, :], R["qw"][:, :], 0.125, None, ALU.mult)
    S.dma(R["b31"][:, :], D["b31"][:, :])
    S.memset(R["Vaug"][:, :, 64:65], 1.0)
    stg = S._wstage
    for di in range(5):
        s = stg[di % 2]
        S.dma(s[:, 0:512], D["biasT"][di, :, :])
        S.copy(R["bhi"][:, di, :], s[:, 0:512], eng="dve")
        S.tt(s[:, 512:1024], s[:, 0:512], R["bhi"][:, di, :], ALU.subtract)
        S.copy(R["blo"][:, di, :], s[:, 512:1024], eng="dve")
    for j in range(8):
        s = stg[j % 2]
        S.dma(s[64:96, 0:1024], D["onehot"][:, j * 1024:(j + 1) * 1024])
        S.copy(R["Kaug"][64:96, j * 1024:(j + 1) * 1024], s[64:96, 0:1024], eng="pool")
    return R


def moba_batch(S, R, load_h, Y, ycol0, ntiles=16):
    Wt, Qaug, Kaug, Vaug, kmean = R["Wt"], R["Qaug"], R["Kaug"], R["Vaug"], R["kmean"]
    for i in range(2):
        S.memset(R["gsb"][i][:, :], -1e30)
        S.memset(R["mask"][i][:, 0:64], 0.0)
        S.memset(R["mask"][i][:, 64:96], -30000.0)
    S.memset(kmean[:, :], 0.0)
    mi = 0
    pi = 0
    for t in range(ntiles):
        hb = load_h(t)
        tsl = slice(t * 512, (t + 1) * 512)
        for s in range(4):
            psv = _psum6(S)
            proj_tm(S, psv[:, 0:64], Wt, 128, 64, hb, s)
            S.copy(Vaug[:, t * 4 + s, 0:64], psv[:, 0:64], eng="act")
        for col0, w, dst, d32 in ((0, R["qw"], Qaug, R["qn32"]), (64, R["kw"], Kaug, R["kn32"])):
            ps = _psum6(S)
            proj_fm(S, ps[0:64, :], Wt, col0, 64, hb)
            S.act(R["sqb"][:, :], ps[0:64, :], AF.Square)
            pss = _psum6(S)
            S.mm(pss[0:64, :], R["ones_bf"][0:64, 0:64], R["sqb"][:, :])
            S.act(R["rs"][:, :], pss[0:64, :], AF.Sqrt, bias=R["eps"][0:64, 0:1], scale=1.0 / 64)
            S.recip(R["rs"][:, :], R["rs"][:, :])
            S.stt(d32[:, :], ps[0:64, :], w[:, 0:1], R["rs"][:, :], ALU.mult, ALU.mult)
            S.copy(dst[0:64, tsl], d32[:, :], eng="pool")
        S.reduce(R["km2"][:, :], R["kn32"][:, :].map(lambda a: a.rearrange("p (n k) -> p n k", k=256)), ALU.add)
        S.ts(kmean[:, 2 * t:2 * t + 2], R["km2"][:, :], 1.0 / 256, None, ALU.mult)
        for s in range(4):
            nb = 2 * t + s // 2
            mb = R["mask"][mi % 2]
            gs = R["gsb"][mi % 2]
            mi += 1
            if nb > 3:
                psg = _psum6(S)
                S.mm(psg[:, 0:32], R["qn32"][:, s * 128:(s + 1) * 128], kmean[:, 0:32])
                S.copy(gs[:, 0:nb], psg[:, 0:nb], eng="dve")
                w8 = max(nb, 8)
                t8 = R["top8"]
                S.op("dve", (lambda o, i: (lambda e: e.max(out=o, in_=i)))(t8.t[:, :], gs.t[:, 0:w8]), reads=[gs], writes=[t8])
                S.ts(mb[:, 64:64 + nb], gs[:, 0:nb], t8[:, 2:3], -30000.0, ALU.is_lt, ALU.mult)
            elif nb > 0:
                S.memset(mb[:, 64:64 + nb], 0.0)
            S.memset(mb[:, 64 + nb:65 + nb], 0.0)
            psm = _psum6(S)
            S.mm(psm[0:96, 0:128], mb[:, 0:96], R["ident"][:, :])
            S.copy(Qaug[64:96, t * 512 + s * 128:t * 512 + (s + 1) * 128], psm[64:96, 0:128], eng="act")
        pso = _acc(S)
        nkt = 4 * t + 4
        def scores(kt):
            pss = _psum6(S)
            near = kt >= 4 * t - 1
            S.mm(pss[:, :], Kaug[0:96, kt * 128:(kt + 1) * 128], Qaug[0:96, tsl], start=True, stop=not near)
            if near:
                di = kt - (4 * t - 1)
                S.mm(pss[:, :], R["ident_bf"][:, :], R["bhi"][:, di, :], start=False, stop=False)
                S.mm(pss[:, :], R["ident_bf"][:, :], R["blo"][:, di, :], start=False, stop=True)
            return pss, near
        LOOK = 2
        pend = [scores(kt) for kt in range(min(LOOK, nkt))]
        for kt in range(nkt):
            pss, near = pend.pop(0)
            if kt + LOOK < nkt:
                pend.append(scores(kt + LOOK))
            pt = R["pt"][pi % 3]
            pi += 1
            S.act(pt[:, :], pss[:, :], AF.Exp, bias=(R["zcol"][:, 0:1] if near else R["b31"][:, 0:1]))
            S.mm(pso[0:65, :], Vaug[:, kt, 0:65], pt[:, :], start=(kt == 0), stop=(kt == nkt - 1))
        S.recip(R["rden"][64:65, :], pso[64:65, :])
        psb = _psum6(S)
        S.mm(psb[0:64, :], R["ones_f"][64:65, 0:64], R["rden"][64:65, :])
        S.copy(R["osb"][:, :], pso[0:64, :], eng="act")
        ya = R["ya"][t % 2]
        S.tt(ya[:, :], R["osb"][:, :], psb[0:64, :], ALU.mult)
        S.dma(Y[0:64, ycol0 + t * 512:ycol0 + (t + 1) * 512], ya[:, :], final=True)


def ssd_consts():
    s = np.arange(128)[:, None]
    t = np.arange(128)[None, :]
    tri = (s <= t).astype(np.float32)
    maskneg = np.where(t >= s, 0.0, -1e4).astype(np.float32)
    return tri, maskneg


def ssd_setup(S, D, C):
    R = {}
    R["Wt"] = load_w_bf(S, "ssd_w", D["wd"], 385)
    R["cw"] = S.sb("ssd_cw", [128, 3, 5], F32)
    R["sc"] = S.sb("ssd_sc", [128, 3], F32)
    S.dma(R["cw"][:, :, :], D["cw"][:, :, :])
    S.dma(R["sc"][:, :], D["sc"][:, :])
    R["negA"] = S.sb("ssd_negA", [128, 1], F32)
    S.act(R["negA"][:, :], R["sc"][:, 1:2], AF.Exp)
    S.ts(R["negA"][:, :], R["negA"][:, :], -1.0, None, ALU.mult)
    R["raw"] = [S.sb(f"ssd_raw{i}", [128, 515], F32) for i in range(3)]
    R["acc"] = S.sb("ssd_acc", [128, 512], F32)
    R["xc32"] = S.sb("ssd_xc32", [64, 512], F32)
    R["xcb"] = S.sb("ssd_xcb", [64, 512], BF16)
    R["Bc"] = S.sb("ssd_Bc", [128, 512], BF16)
    R["Cc"] = S.sb("ssd_Cc", [128, 512], BF16)
    R["zs"] = S.sb("ssd_zs", [64, 512], F32)
    R["e4"] = S.sb("ssd_e4", [128, 4], F32)
    R["dt4"] = S.sb("ssd_dt4", [128, 4], F32)
    R["dtA4"] = S.sb("ssd_dtA4", [128, 4], F32)
    R["acs4"] = S.sb("ssd_acs4", [128, 4], F32)
    R["dec4"] = S.sb("ssd_dec4", [128, 4], F32)
    R["cd4"] = S.sb("ssd_cd4", [128, 4], F32)
    R["xs4"] = S.sb("ssd_xs4", [128, 4], F32)
    R["xg"] = [S.sb(f"ssd_xg{i}", [128, 64], BF16) for i in range(2)]
    R["xgd"] = [S.sb(f"ssd_xgd{i}", [128, 64], BF16) for i in range(2)]
    R["Btm"] = [S.sb(f"ssd_Btm{i}", [128, 128], BF16) for i in range(2)]
    R["rhsc"] = [S.sb(f"ssd_rhsc{i}", [128, 128], F32) for i in range(2)]
    R["tmp"] = [S.sb(f"ssd_tmp{i}", [128, 128], F32) for i in range(2)]
    R["LT"] = [S.sb(f"ssd_LT{i}", [128, 128], F32) for i in range(2)]
    R["Er"] = [S.sb(f"ssd_Er{i}", [128, 128], BF16) for i in range(2)]
    R["MT"] = [S.sb(f"ssd_MT{i}", [128, 128], BF16) for i in range(2)]
    R["Cp"] = [S.sb(f"ssd_Cp{i}", [128, 128], BF16) for i in range(2)]
    R["yv"] = [S.sb(f"ssd_yv{i}", [64, 128], F32) for i in range(2)]
    R["yout"] = [S.sb(f"ssd_yout{i}", [64, 512], BF16) for i in range(2)]
    R["h32"] = S.sb("ssd_h32", [128, 64], F32)
    R["hbf"] = S.sb("ssd_hbf", [128, 64], BF16)
    R.update(C)
    return R


def ssd_batch(S, R, load_h, Y, yrow0, ycol0, ntiles=16):
    Wt = R["Wt"]
    cw, sc = R["cw"], R["sc"]
    for i in range(3):
        S.memset(R["raw"][i][:, 0:3], 0.0)
    S.memset(R["h32"][:, :], 0.0)
    S.memset(R["hbf"][:, :], 0.0)
    n2 = 0
    for t in range(ntiles):
        hb = load_h(t)
        for gi, (col0, M, dst32, dstb) in enumerate(((0, 64, R["xc32"], R["xcb"]), (128, 128, None, R["Bc"]), (256, 128, None, R["Cc"]))):
            raw = R["raw"][gi]
            ps = _psum6(S)
            proj_fm(S, ps[0:M, :], Wt, col0, M, hb)
            S.copy(raw[0:M, 3:515], ps[0:M, :], eng="act")
            acc = R["acc"]
            S.ts(acc[0:M, :], raw[0:M, 0:512], cw[0:M, gi, 0:1], cw[0:M, gi, 4:5], ALU.mult, ALU.add)
            for j in range(1, 4):
                S.stt(acc[0:M, :], raw[0:M, j:j + 512], cw[0:M, gi, j:j + 1], acc[0:M, :], ALU.mult, ALU.add)
            S.copy(raw[0:M, 0:3], raw[0:M, 512:515], eng="pool")
            if dst32 is not None:
                S.act(dst32[0:M, :], acc[0:M, :], AF.Silu)
                S.copy(dstb[0:M, :], dst32[0:M, :], eng="pool")
            else:
                S.act(dstb[0:M, :], acc[0:M, :], AF.Silu)
        ps = _psum6(S)
        proj_fm(S, ps[0:64, :], Wt, 64, 64, hb)
        S.act(R["zs"][:, :], ps[0:64, :], AF.Silu)
        psd = _psum6(S)
        for j in range(4):
            proj_tm(S, psd[:, j:j + 1], Wt, 384, 1, hb, j)
        S.act(R["e4"][:, :], psd[:, 0:4], AF.Exp, bias=sc[:, 0:1])
        S.act(R["dt4"][:, :], R["e4"][:, :], AF.Ln, bias=R["onecol"][:, 0:1])
        S.ts(R["dtA4"][:, :], R["dt4"][:, :], R["negA"][:, 0:1], None, ALU.mult)
        psa = _psum6(S)
        S.mm(psa[:, 0:4], R["tri"][:, :], R["dtA4"][:, :])
        S.mm(psa[:, 4:8], R["ones_f"][:, :], R["dtA4"][:, :])
        S.copy(R["acs4"][:, :], psa[:, 0:4], eng="dve")
        S.tt(R["dec4"][:, :], psa[:, 4:8], R["acs4"][:, :], ALU.subtract)
        S.act(R["dec4"][:, :], R["dec4"][:, :], AF.Exp)
        S.act(R["cd4"][:, :], psa[:, 4:8], AF.Exp)
        S.tt(R["xs4"][:, :], R["dt4"][:, :], R["dec4"][:, :], ALU.mult)
        yout = R["yout"][t % 2]
        for j in range(4):
            csl = slice(j * 128, (j + 1) * 128)
            k = n2 % 2
            n2 += 1
            pst = _psum6(S)
            S.mm(pst[:, 0:64], R["xcb"][:, csl], R["ident_bf"][0:64, 0:64])
            S.mm(pst[:, 64:192], R["Bc"][:, csl], R["ident_bf"][:, :])
            S.ts(R["xg"][k][:, :], pst[:, 0:64], R["dt4"][:, j:j + 1], None, ALU.mult)
            S.ts(R["xgd"][k][:, :], pst[:, 0:64], R["xs4"][:, j:j + 1], None, ALU.mult)
            S.copy(R["Btm"][k][:, :], pst[:, 64:192], eng="act")
            S.ts(R["rhsc"][k][:, :], R["tri"][:, :], R["dtA4"][:, j:j + 1], None, ALU.mult, eng="pool")
            psr = _psum6(S)
            S.mm(psr[:, 0:128], R["ones_f"][:, :], R["rhsc"][k][:, :])
            S.ts(R["tmp"][k][:, :], psr[:, 0:128], R["acs4"][:, j:j + 1], 0.0, ALU.subtract, ALU.min)
            S.tt(R["tmp"][k][:, :], R["tmp"][k][:, :], R["maskneg"][:, :], ALU.add, eng="pool")
            S.act(R["LT"][k][:, :], R["tmp"][k][:, :], AF.Exp)
            S.act(R["Er"][k][:, :], psr[:, 0:128], AF.Exp)
            psc = _psum6(S)
            S.mm(psc[:, 0:128], R["Bc"][:, csl], R["Cc"][:, csl])
            S.tt(R["MT"][k][:, :], psc[:, 0:128], R["LT"][k][:, :], ALU.mult)
            S.tt(R["Cp"][k][:, :], R["Cc"][:, csl], R["Er"][k][:, :], ALU.mult, eng="pool")
            psy = _psum6(S)
            S.mm(psy[0:64, 0:128], R["xg"][k][:, :], R["MT"][k][:, :], start=True, stop=False)
            S.mm(psy[0:64, 0:128], R["hbf"][:, :], R["Cp"][k][:, :], start=False, stop=True)
            S.stt(R["yv"][k][:, :], R["xc32"][:, csl], sc[0:64, 2:3], psy[0:64, 0:128], ALU.mult, ALU.add)
            S.tt(yout[:, csl], R["yv"][k][:, :], R["zs"][:, csl], ALU.mult)
            pss = _psum6(S)
            S.mm(pss[:, 0:64], R["Btm"][k][:, :], R["xgd"][k][:, :])
            S.stt(R["h32"][:, :], R["h32"][:, :], R["cd4"][:, j:j + 1], pss[:, 0:64], ALU.mult, ALU.add)
            S.copy(R["hbf"][:, :], R["h32"][:, :], eng="act")
        S.dma(Y[yrow0:yrow0 + 64, ycol0 + t * 512:ycol0 + (t + 1) * 512], yout[:, :], final=True)


def gdn_consts():
    j = np.arange(64)[:, None]
    i = np.arange(64)[None, :]
    incl = np.where(i >= j, 0.0, -1e4).astype(np.float32)
    ns = np.where(i > j, -1.0, 0.0).astype(np.float32)
    return incl, ns


def neumann_inverse(S, R, N, NT, tag, RR=None, PW=None):
    RR = RR if RR is not None else R["RR"]
    PW = PW if PW is not None else R["PW"]
    idt = R["ident"]
    S.tt(RR[0:64, 0:64], N, idt[0:64, 0:64], ALU.add)
    S.tt(RR[0:64, 64:128], NT, idt[0:64, 0:64], ALU.add, eng="pool")
    curN, curNT = N, NT
    for lvl in range(5):
        pw = PW[lvl % 2]
        ps = _psum6(S)
        S.mm(ps[0:64, 0:64], curNT, curN)
        S.mm(ps[0:64, 64:128], curN, curNT)
        S.copy(pw[0:64, 0:128], ps[0:64, 0:128], eng="act")
        curN, curNT = pw[0:64, 0:64], pw[0:64, 64:128]
        ps2 = _psum6(S)
        S.mm(ps2[0:64, 0:64], RR[0:64, 64:128], curN)
        S.mm(ps2[0:64, 64:128], RR[0:64, 0:64], curNT)
        S.tt(RR[0:64, 0:128], RR[0:64, 0:128], ps2[0:64, 0:128], ALU.add)
    return RR[0:64, 0:64], RR[0:64, 64:128]


def gdn_setup(S, D, C):
    R = {}
    R["Wt"] = load_w_bf(S, "gdn_w", D["wg"], 386)
    R["cw"] = S.sb("gdn_cw", [64, 3, 4], F32)
    R["sc"] = S.sb("gdn_sc", [128, 3], F32)
    R["nw"] = S.sb("gdn_nw", [64, 1], F32)
    S.dma(R["cw"][:, :, :], D["cw"][:, :, :])
    S.dma(R["sc"][:, :], D["sc"][:, :])
    S.dma(R["nw"][:, :], D["nw"][:, :])
    R["negA"] = S.sb("gdn_negA", [128, 1], F32)
    S.act(R["negA"][:, :], R["sc"][:, 0:1], AF.Exp)
    S.ts(R["negA"][:, :], R["negA"][:, :], -1.0, None, ALU.mult)
    R["raw"] = [S.sb(f"gdn_raw{i}", [64, 515], F32) for i in range(3)]
    R["acc"] = S.sb("gdn_acc", [64, 512], F32)
    R["q32"] = S.sb("gdn_q32", [64, 512], F32)
    R["k32"] = S.sb("gdn_k32", [64, 512], F32)
    R["vcb"] = S.sb("gdn_vcb", [64, 512], BF16)
    R["sqb"] = S.sb("gdn_sqb", [64, 512], BF16)
    R["rs"] = S.sb("gdn_rs", [64, 512], F32)
    R["qnb"] = S.sb("gdn_qnb", [64, 512], BF16)
    R["knb"] = S.sb("gdn_knb", [64, 512], BF16)
    R["kbT"] = S.sb("gdn_kbT", [64, 512], BF16)
    R["qdT"] = S.sb("gdn_qdT", [64, 512], BF16)
    R["zs"] = S.sb("gdn_zs", [64, 512], F32)
    R["GC"] = S.sb("gdn_GC", [64, 512], F32)
    R["BETA"] = S.sb("gdn_BETA", [64, 512], F32)
    R["E"] = S.sb("gdn_E", [64, 512], F32)
    R["col"] = S.sb("gdn_col", [64, 6, 8], F32)
    R["dT"] = [S.sb(f"gdn_dT{i}", [64, 64], F32) for i in range(8)]
    R["dTn"] = [S.sb(f"gdn_dTn{i}", [64, 64], F32) for i in range(8)]
    R["NN"] = [S.sb(f"gdn_NN{i}", [64, 128], F32) for i in range(8)]
    R["RR"] = [S.sb(f"gdn_RR{i}", [64, 128], F32) for i in range(8)]
    R["PW"] = [[S.sb(f"gdn_PW{i}_{j}", [64, 128], F32) for j in range(2)] for i in range(8)]
    R["tm"] = [S.sb(f"gdn_tm{i}", [64, 3, 64], F32) for i in range(8)]
    R["kdec"] = [S.sb(f"gdn_kdec{i}", [64, 64], BF16) for i in range(8)]
    R["ain"] = [S.sb(f"gdn_ain{i}", [64, 64], BF16) for i in range(8)]
    R["u32"] = [S.sb(f"gdn_u32{i}", [64, 64], F32) for i in range(8)]
    R["wT"] = [S.sb(f"gdn_wT{i}", [64, 64], BF16) for i in range(8)]
    R["vn"] = [S.sb(f"gdn_vn{i}", [64, 64], BF16) for i in range(8)]
    R["st32"] = S.sb("gdn_st32", [64, 64], F32)
    R["stb"] = S.sb("gdn_stb", [64, 64], BF16)
    R["ot"] = [S.sb(f"gdn_ot{i}", [64, 512], F32) for i in range(2)]
    R["yout"] = [S.sb(f"gdn_yout{i}", [64, 512], BF16) for i in range(2)]
    R.update(C)
    return R


def gdn_batch(S, R, load_h, Y, yrow0, ycol0, ntiles=16, stop=99):
    Wt, cw, sc = R["Wt"], R["cw"], R["sc"]
    col = R["col"]
    for i in range(3):
        S.memset(R["raw"][i][:, 0:3], 0.0)
    S.memset(R["st32"][:, :], 0.0)
    S.memset(R["stb"][:, :], 0.0)
    n2 = 0
    for t in range(ntiles):
        hb = load_h(t)
        for gi, dst in enumerate((R["q32"], R["k32"], R["vcb"])):
            raw = R["raw"][gi]
            ps = _psum6(S)
            proj_fm(S, ps[0:64, :], Wt, gi * 64, 64, hb)
            S.copy(raw[:, 3:515], ps[0:64, :], eng="act")
            acc = R["acc"]
            S.ts(acc[:, :], raw[:, 0:512], cw[:, gi, 0:1], None, ALU.mult)
            for j in range(1, 4):
                S.stt(acc[:, :], raw[:, j:j + 512], cw[:, gi, j:j + 1], acc[:, :], ALU.mult, ALU.add)
            S.copy(raw[:, 0:3], raw[:, 512:515], eng="pool")
            S.act(dst[:, :], acc[:, :], AF.Silu)
        ps = _psum6(S)
        proj_fm(S, ps[0:64, :], Wt, 192, 64, hb)
        S.act(R["zs"][:, :], ps[0:64, :], AF.Silu)
        for src, dstb, scl in ((R["q32"], R["qnb"], 0.125), (R["k32"], R["knb"], 1.0)):
            S.act(R["sqb"][:, :], src[:, :], AF.Square)
            pss = _psum6(S)
            S.mm(pss[0:64, :], R["ones_bf"][0:64, 0:64], R["sqb"][:, :])
            S.act(R["rs"][:, :], pss[0:64, :], AF.Sqrt, bias=R["eps6"][0:64, 0:1], scale=1.0)
            S.recip(R["rs"][:, :], R["rs"][:, :])
            S.stt(src[:, :], src[:, :], scl, R["rs"][:, :], ALU.mult, ALU.mult)
            S.copy(dstb[:, :], src[:, :], eng="pool")
        psbt = _psum6(S)
        proj_fm(S, psbt[0:64, :], Wt, 258, 64, hb)
        S.act(R["BETA"][:, :], psbt[0:64, :], AF.Sigmoid)
        S.tt(R["kbT"][:, :], R["k32"][:, :], R["BETA"][:, :], ALU.mult)
        psa = _psum6(S)
        proj_fm(S, psa[0:64, :], Wt, 322, 64, hb)
        S.act(R["E"][:, :], psa[0:64, :], AF.Exp, bias=sc[0:64, 1:2])
        S.act(R["E"][:, :], R["E"][:, :], AF.Ln, bias=R["onecol"][0:64, 0:1])
        S.ts(R["E"][:, :], R["E"][:, :], R["negA"][0:64, 0:1], None, ALU.mult)
        for c in range(8):
            cs = slice(c * 64, (c + 1) * 64)
            S.scan(R["GC"][:, cs], R["ones_f"][0:64, 0:64], R["E"][:, cs], 0.0, ALU.mult, ALU.add)
        S.act(R["E"][:, :], R["GC"][:, :], AF.Exp)
        S.tt(R["qdT"][:, :], R["q32"][:, :], R["E"][:, :], ALU.mult)
        psc = _psum6(S)
        for c in range(8):
            for kc in range(8):
                S.mm(psc[0:64, 2 * c:2 * c + 2], hb[:, kc, c * 64:(c + 1) * 64], Wt[:, kc, 256:258], start=(kc == 0), stop=(kc == 7))
        pv = psc[0:64, 0:16].map(lambda a: a.rearrange("p (c two) -> p two c", two=2))
        S.act(col[:, 1, :], pv[:, 0, :], AF.Sigmoid)
        S.act(col[:, 5, :], pv[:, 1, :], AF.Exp, bias=sc[0:64, 1:2])
        S.act(col[:, 5, :], col[:, 5, :], AF.Ln, bias=R["onecol"][0:64, 0:1])
        S.ts(col[:, 5, :], col[:, 5, :], R["negA"][0:64, 0:1], None, ALU.mult)
        psq = _psum6(S)
        S.mm(psq[0:64, 0:8], R["tri"][0:64, 0:64], col[:, 5, :])
        S.copy(col[:, 0, :], psq[0:64, 0:8], eng="dve")
        S.act(col[:, 5, :], col[:, 0, :], AF.Exp)
        S.tt(col[:, 2, :], col[:, 1, :], col[:, 5, :], ALU.mult)
        gl8 = R["GC"][:, :].map(lambda a: a.rearrange("p (c k) -> p c k", k=64)[:, :, 63])
        S.tt(col[:, 3, :], gl8, col[:, 0, :], ALU.subtract)
        S.act(col[:, 3, :], col[:, 3, :], AF.Exp)
        S.act(col[:, 4, :], gl8, AF.Exp)
        if stop <= 6:
            continue
        ot = R["ot"][t % 2]
        def par(c):
            cs = slice(c * 64, (c + 1) * 64)
            k = c
            pst = _psum6(S)
            S.mm(pst[0:64, 0:64], R["knb"][:, cs], R["ident_bf"][0:64, 0:64])
            S.mm(pst[0:64, 64:128], R["vcb"][:, cs], R["ident_bf"][0:64, 0:64])
            tm = R["tm"][k]
            S.ts(tm[:, 0, :], pst[0:64, 64:128], col[:, 1, c:c + 1], None, ALU.mult)
            S.ts(tm[:, 1, :], pst[0:64, 0:64], col[:, 2, c:c + 1], None, ALU.mult)
            S.ts(R["kdec"][k][:, :], pst[0:64, 0:64], col[:, 3, c:c + 1], None, ALU.mult)
            psA = _psum6(S)
            S.mm(psA[0:64, 0:64], R["knb"][:, cs], R["kbT"][:, cs])
            S.mm(psA[0:64, 64:128], R["knb"][:, cs], R["qnb"][:, cs])
            dT = R["dT"][k]
            S.ts(dT[:, :], R["GC"][:, cs], col[:, 0, c:c + 1], 0.0, ALU.subtract, ALU.min)
            S.tt(dT[:, :], dT[:, :], R["gm_incl"][0:64, 0:64], ALU.add, eng="pool")
            S.act(dT[:, :], dT[:, :], AF.Exp)
            S.tt(R["dTn"][k][:, :], dT[:, :], R["gm_ns"][0:64, 0:64], ALU.mult, eng="pool")
            NN = R["NN"][k]
            S.tt(NN[:, 0:64], psA[0:64, 0:64], R["dTn"][k][:, :], ALU.mult)
            S.tt(R["ain"][k][:, :], psA[0:64, 64:128], dT[:, :], ALU.mult)
            psn = _psum6(S)
            S.mm(psn[0:64, 0:64], NN[:, 0:64], R["ident"][0:64, 0:64])
            S.copy(NN[:, 64:128], psn[0:64, 0:64], eng="act")
            TT, Tm = neumann_inverse(S, R, NN[:, 0:64], NN[:, 64:128], "g", RR=R["RR"][k], PW=R["PW"][k])
            psU = _psum6(S)
            S.mm(psU[0:64, 0:64], TT, tm[:, 0, :])
            S.mm(psU[0:64, 64:128], tm[:, 1, :], TT)
            S.copy(R["u32"][k][:, :], psU[0:64, 0:64], eng="act")
            S.copy(R["wT"][k][:, :], psU[0:64, 64:128], eng="dve")
        run_interleaved(S, [(lambda cc: (lambda: par(cc)))(c) for c in range(8)], width=3)
        for c in range(8):
            cs = slice(c * 64, (c + 1) * 64)
            k = c
            psV = _psum6(S)
            S.mm(psV[0:64, 0:64], R["wT"][k][:, :], R["stb"][:, :])
            S.tt(R["vn"][k][:, :], R["u32"][k][:, :], psV[0:64, 0:64], ALU.subtract)
            psO = _psum6(S)
            S.mm(psO[0:64, 0:64], R["stb"][:, :], R["qdT"][:, cs], start=True, stop=False)
            S.mm(psO[0:64, 0:64], R["vn"][k][:, :], R["ain"][k][:, :], start=False, stop=True)
            S.mm(psO[0:64, 64:128], R["kdec"][k][:, :], R["vn"][k][:, :])
            S.stt(R["st32"][:, :], R["st32"][:, :], col[:, 4, c:c + 1], psO[0:64, 64:128], ALU.mult, ALU.add)
            S.copy(R["stb"][:, :], R["st32"][:, :], eng="act")
            S.copy(ot[:, cs], psO[0:64, 0:64], eng="act")
        if stop <= 12:
            continue
        S.act(R["sqb"][:, :], ot[:, :], AF.Square)
        pss = _psum6(S)
        S.mm(pss[0:64, :], R["ones_bf"][0:64, 0:64], R["sqb"][:, :])
        S.act(R["rs"][:, :], pss[0:64, :], AF.Sqrt, bias=R["eps6"][0:64, 0:1], scale=1.0 / 64)
        S.recip(R["rs"][:, :], R["rs"][:, :])
        S.stt(ot[:, :], ot[:, :], R["nw"][:, 0:1], R["rs"][:, :], ALU.mult, ALU.mult)
        yout = R["yout"][t % 2]
        S.tt(yout[:, :], ot[:, :], R["zs"][:, :], ALU.mult)
        S.dma(Y[yrow0:yrow0 + 64, ycol0 + t * 512:ycol0 + (t + 1) * 512], yout[:, :], final=True)


def rwkv_consts():
    s = np.arange(64)[:, None]
    t = np.arange(64)[None, :]
    strict = (t > s).astype(np.float32)
    incl = (t >= s).astype(np.float32)
    return np.concatenate([strict, incl, strict, incl], axis=1)


def rwkv_host(i, h, d):
    base = 3600
    hc = h * 64 + np.arange(64)
    cols = np.concatenate([base + hc, base + 512 + hc, base + 1024 + hc, base + 1536 + np.arange(64), base + 1600 + np.arange(64),
                           base + 1664 + np.arange(128)])
    ncol = 448
    if i > 0:
        cols = np.concatenate([cols, base + 1024 + np.arange(512)])
        ncol = 960
    wr = np.ascontiguousarray(d["w_in"][i][:, cols].reshape(8, 128, ncol).transpose(1, 0, 2))
    mu = d["rwkv_mu"][i]
    pc = np.zeros((128, 16), np.float32)
    pc[:64, 0] = mu[hc]; pc[:64, 1] = mu[512 + hc]; pc[:64, 2] = mu[1024 + hc]
    pc[:64, 3] = mu[1536:1600]; pc[:64, 4] = mu[1600:1664]; pc[:, 5] = mu[1664:1792]
    pc[:64, 6] = d["rwkv_w0"][i][hc]; pc[:64, 7] = d["rwkv_a0"][i][hc]
    pc[:64, 8] = d["rwkv_k_k"][i][hc]; pc[:64, 9] = d["rwkv_k_a"][i][hc]
    pc[:64, 10] = d["rwkv_ln_w"][i][hc]; pc[:64, 11] = d["rwkv_ln_b"][i][hc]
    if i > 0:
        pc[:64, 12] = d["rwkv_v0"][i - 1][hc]
        pc[:, 13] = 0
    mats = np.zeros((128, 4, 64), np.float32)
    mats[:64, 0, :] = d["rwkv_w_up"][i][:, hc]
    mats[:64, 1, :] = d["rwkv_a_up"][i][:, hc]
    mats[:, 2, :] = d["rwkv_g_up"][i][:, hc]
    mats[:64, 3, :] = np.repeat(d["rwkv_r_k"][i][h][:, None], 64, axis=1)
    out = {"wr": wr, "pc": pc, "mats": mats}
    if i > 0:
        out["muv"] = np.ascontiguousarray(mu[1024:1536].reshape(4, 128).T)
        out["vdown"] = np.ascontiguousarray(d["rwkv_v_down"][i - 1].reshape(4, 128, 32).transpose(1, 0, 2))
        vu = np.zeros((32, 64), np.float32); vu[:, :] = d["rwkv_v_up"][i - 1][:, hc]
        out["vup"] = vu
    return out


def rwkv_setup(S, D, C, layer1):
    R = {"layer1": layer1}
    ncol = 960 if layer1 else 448
    R["Wt"] = load_w_bf(S, "rw_w", D["wr"], ncol)
    R["pc"] = S.sb("rw_pc", [128, 16], F32)
    S.dma(R["pc"][:, :], D["pc"][:, :])
    R["mats32"] = S.sb("rw_mats32", [128, 4, 64], F32)
    S.dma(R["mats32"][:, :, :], D["mats"][:, :, :])
    R["mats"] = S.sb("rw_mats", [128, 4, 64], BF16)
    S.copy(R["mats"][:, :, :], R["mats32"][:, :, :], eng="pool")
    R["omka"] = S.sb("rw_omka", [64, 1], F32)
    S.ts(R["omka"][:, :], R["pc"][0:64, 9:10], -1.0, 1.0, ALU.mult, ALU.add)
    R["m4"] = S.sb("rw_m4", [64, 256], F32)
    S.dma(R["m4"][:, :], D["m4"][:, :])
    if layer1:
        R["muv"] = S.sb("rw_muv", [128, 4], F32)
        S.dma(R["muv"][:, :], D["muv"][:, :])
        R["vd32"] = S.sb("rw_vd32", [128, 4, 32], F32)
        S.dma(R["vd32"][:, :, :], D["vdown"][:, :, :])
        R["vdown"] = S.sb("rw_vdown", [128, 4, 32], BF16)
        S.copy(R["vdown"][:, :, :], R["vd32"][:, :, :], eng="pool")
        R["vu32"] = S.sb("rw_vu32", [32, 64], F32)
        S.dma(R["vu32"][:, :], D["vup"][:, :])
        R["vup"] = S.sb("rw_vup", [32, 64], BF16)
        S.copy(R["vup"][:, :], R["vu32"][:, :], eng="pool")
        R["rawvf"] = [S.sb(f"rw_rawvf{g}", [128, 513], F32) for g in range(4)]
        R["cvf"] = S.sb("rw_cvf", [128, 512], BF16)
        R["vl"] = S.sb("rw_vl", [32, 512], BF16)
        R["lam"] = S.sb("rw_lam", [64, 512], F32)
        R["vft"] = S.sb("rw_vft", [64, 512], F32)
    names = ["r", "k", "v", "wd", "ad"]
    R["raw"] = {n: S.sb(f"rw_raw_{n}", [64, 513], F32) for n in names}
    R["rawg"] = S.sb("rw_raw_gd", [128, 513], F32)
    R["c"] = {n: S.sb(f"rw_c_{n}", [64, 512], F32) for n in names}
    R["cg"] = S.sb("rw_c_gd", [128, 512], F32)
    R["dif"] = S.sb("rw_dif", [128, 512], F32)
    R["tb"] = S.sb("rw_tb", [128, 512], BF16)
    R["lw"] = S.sb("rw_lw", [64, 512], F32)
    R["a32"] = S.sb("rw_a32", [64, 512], F32)
    R["gout"] = S.sb("rw_gout", [64, 512], F32)
    R["kk"] = S.sb("rw_kk", [64, 512], F32)
    R["kr"] = S.sb("rw_kr", [64, 512], F32)
    R["sqb"] = S.sb("rw_sqb", [64, 512], BF16)
    R["rs"] = S.sb("rw_rs", [64, 512], F32)
    R["cum"] = S.sb("rw_cum", [64, 512], F32)
    R["Ecp"] = S.sb("rw_Ecp", [64, 512], F32)
    R["Ecm"] = S.sb("rw_Ecm", [64, 512], F32)
    R["Ecs"] = S.sb("rw_Ecs", [64, 512], F32)
    R["ABR"] = S.sb("rw_ABR", [64, 8, 128], BF16)
    R["bb"] = S.sb("rw_bb", [64, 512], BF16)
    R["kb"] = S.sb("rw_kb", [64, 512], BF16)
    R["rbb"] = S.sb("rw_rbb", [64, 512], BF16)
    R["vb"] = S.sb("rw_vb", [64, 512], BF16)
    R["abf"] = S.sb("rw_abf", [64, 512], BF16)
    R["BKt"] = S.sb("rw_BKt", [64, 2, 512], BF16)
    R["wc8"] = S.sb("rw_wc8", [64, 8], F32)
    R["NN"] = [S.sb(f"rw_NN{i}", [64, 128], F32) for i in range(8)]
    R["SCb"] = [S.sb(f"rw_SCb{i}", [64, 192], BF16) for i in range(8)]
    R["RR"] = [S.sb(f"rw_RR{i}", [64, 128], F32) for i in range(8)]
    R["PW"] = [[S.sb(f"rw_PW{i}_{j}", [64, 128], F32) for j in range(2)] for i in range(8)]
    R["tm"] = [S.sb(f"rw_tm{i}", [64, 4, 64], BF16) for i in range(8)]
    R["atm32"] = [S.sb(f"rw_atm32{i}", [64, 64], F32) for i in range(8)]
    R["x1"] = [S.sb(f"rw_x1{i}", [64, 64], F32) for i in range(8)]
    R["u32"] = [S.sb(f"rw_u32{i}", [64, 64], F32) for i in range(8)]
    R["wmT"] = [S.sb(f"rw_wmT{i}", [64, 64], BF16) for i in range(8)]
    R["Db"] = [S.sb(f"rw_Db{i}", [64, 64], BF16) for i in range(8)]
    R["H32"] = S.sb("rw_H32", [64, 64], F32)
    R["Hb"] = S.sb("rw_Hb", [64, 64], BF16)
    R["yt"] = [S.sb(f"rw_yt{i}", [64, 512], F32) for i in range(2)]
    R["yb"] = S.sb("rw_yb", [64, 512], BF16)
    R["yc"] = S.sb("rw_yc", [64, 512], F32)
    R["pr"] = S.sb("rw_pr", [64, 512], BF16)
    R["bon"] = S.sb("rw_bon", [64, 512], F32)
    R["eps_ln"] = S.sb("rw_epsln", [64, 1], F32)
    S.memset(R["eps_ln"][:, :], 64e-5)
    R["yout"] = [S.sb(f"rw_yout{i}", [64, 512], BF16) for i in range(2)]
    R.update(C)
    return R


def rwkv_batch(S, R, load_h, Y, yrow0, ycol0, VF, ntiles=16):
    Wt, pc, mats = R["Wt"], R["pc"], R["mats"]
    layer1 = R["layer1"]
    names = ["r", "k", "v", "wd", "ad"]
    for n in names:
        S.memset(R["raw"][n][:, 0:1], 0.0)
    S.memset(R["rawg"][:, 0:1], 0.0)
    if layer1:
        for g in range(4):
            S.memset(R["rawvf"][g][:, 0:1], 0.0)
    S.memset(R["H32"][:, :], 0.0)
    S.memset(R["Hb"][:, :], 0.0)
    n2 = 0
    for t in range(ntiles):
        hb = load_h(t)
        tsl = slice(ycol0 + t * 512, ycol0 + (t + 1) * 512)
        def lerp(raw, cdst, M, mucol, col0):
            ps = _psum6(S)
            proj_fm(S, ps[0:M, :], Wt, col0, M, hb)
            S.copy(raw[0:M, 1:513], ps[0:M, :], eng="act")
            S.tt(R["dif"][0:M, :], raw[0:M, 0:512], raw[0:M, 1:513], ALU.subtract, eng="pool")
            S.stt(cdst, R["dif"][0:M, :], mucol, raw[0:M, 1:513], ALU.mult, ALU.add)
            S.copy(raw[0:M, 0:1], raw[0:M, 512:513], eng="pool")
        for gi, n in enumerate(names):
            lerp(R["raw"][n], R["c"][n][:, :], 64, pc[0:64, gi:gi + 1], gi * 64)
        lerp(R["rawg"], R["cg"][:, :], 128, pc[:, 5:6], 320)
        c = R["c"]
        S.act(R["tb"][0:64, :], c["wd"][:, :], AF.Tanh)
        ps = _psum6(S)
        S.mm(ps[0:64, :], mats[0:64, 0, :], R["tb"][0:64, :])
        S.act(R["lw"][:, :], ps[0:64, :], AF.Sigmoid, bias=pc[0:64, 6:7])
        S.ts(R["lw"][:, :], R["lw"][:, :], -0.6065306597126334, None, ALU.mult)
        S.copy(R["tb"][0:64, :], c["ad"][:, :], eng="pool")
        ps = _psum6(S)
        S.mm(ps[0:64, :], mats[0:64, 1, :], R["tb"][0:64, :])
        S.act(R["a32"][:, :], ps[0:64, :], AF.Sigmoid, bias=pc[0:64, 7:8])
        S.act(R["tb"][:, :], R["cg"][:, :], AF.Sigmoid)
        ps = _psum6(S)
        S.mm(ps[0:64, :], mats[:, 2, :], R["tb"][:, :])
        S.copy(R["gout"][:, :], ps[0:64, :], eng="act")
        if not layer1:
            S.dma(VF[0:64, tsl], c["v"][:, :])
        else:
            psl = _acc(S)
            for g in range(4):
                ps = _psum6(S)
                proj_fm(S, ps[:, :], Wt, 448 + g * 128, 128, hb)
                raw = R["rawvf"][g]
                S.copy(raw[:, 1:513], ps[:, :], eng="act")
                S.tt(R["dif"][:, :], raw[:, 0:512], raw[:, 1:513], ALU.subtract, eng="pool")
                S.stt(R["cvf"][:, :], R["dif"][:, :], R["muv"][:, g:g + 1], raw[:, 1:513], ALU.mult, ALU.add)
                S.copy(raw[:, 0:1], raw[:, 512:513], eng="pool")
                S.mm(psl[0:32, :], R["vdown"][:, g, :], R["cvf"][:, :], start=(g == 0), stop=(g == 3))
            S.copy(R["vl"][:, :], psl[0:32, :], eng="act")
            ps = _psum6(S)
            S.mm(ps[0:64, :], R["vup"][:, :], R["vl"][:, :])
            S.act(R["lam"][:, :], ps[0:64, :], AF.Sigmoid, bias=pc[0:64, 12:13])
            S.dma(R["vft"][:, :], VF[0:64, tsl])
            S.tt(R["vft"][:, :], R["vft"][:, :], c["v"][:, :], ALU.subtract)
            S.tt(R["vft"][:, :], R["vft"][:, :], R["lam"][:, :], ALU.mult)
            S.tt(c["v"][:, :], c["v"][:, :], R["vft"][:, :], ALU.add)
        S.ts(R["kk"][:, :], c["k"][:, :], pc[0:64, 8:9], None, ALU.mult)
        S.act(R["sqb"][:, :], R["kk"][:, :], AF.Square)
        pss = _psum6(S)
        S.mm(pss[0:64, :], R["ones_bf"][0:64, 0:64], R["sqb"][:, :])
        S.act(R["rs"][:, :], pss[0:64, :], AF.Sqrt, bias=R["eps6"][0:64, 0:1], scale=1.0)
        S.recip(R["rs"][:, :], R["rs"][:, :])
        S.tt(R["kk"][:, :], R["kk"][:, :], R["rs"][:, :], ALU.mult)
        S.ts(R["kr"][:, :], R["a32"][:, :], pc[0:64, 9:10], R["omka"][:, 0:1], ALU.mult, ALU.add)
        S.tt(R["kr"][:, :], R["kr"][:, :], c["k"][:, :], ALU.mult)
        for cc in range(8):
            cs = slice(cc * 64, (cc + 1) * 64)
            S.scan(R["cum"][:, cs], R["ones_f"][0:64, 0:64], R["lw"][:, cs], 0.0, ALU.mult, ALU.add)
        S.act(R["Ecp"][:, :], R["cum"][:, :], AF.Exp)
        S.act(R["Ecm"][:, :], R["cum"][:, :], AF.Exp, scale=-1.0)
        S.tt(R["Ecs"][:, :], R["cum"][:, :], R["lw"][:, :], ALU.subtract)
        S.act(R["Ecs"][:, :], R["Ecs"][:, :], AF.Exp)
        abr = R["ABR"]
        v3 = lambda vw: vw.map(lambda a: a.rearrange("p (c k) -> p c k", k=64))
        S.stt(R["abf"][:, :], R["kk"][:, :], -1.0, R["Ecs"][:, :], ALU.mult, ALU.mult)
        S.copy(abr[:, :, 0:64], v3(R["abf"][:, :]), eng="pool")
        S.tt(R["rbb"][:, :], c["r"][:, :], R["Ecp"][:, :], ALU.mult)
        S.copy(abr[:, :, 64:128], v3(R["rbb"][:, :]), eng="pool")
        S.tt(R["dif"][0:64, :], R["kk"][:, :], R["a32"][:, :], ALU.mult)
        S.tt(R["bb"][:, :], R["dif"][0:64, :], R["Ecm"][:, :], ALU.mult)
        S.tt(R["kb"][:, :], R["kr"][:, :], R["Ecm"][:, :], ALU.mult)
        S.copy(R["vb"][:, :], c["v"][:, :], eng="pool")
        S.copy(R["wc8"][:, :], v3(R["Ecp"][:, :])[:, :, 63], eng="dve")
        for cc in range(8):
            cs = slice(cc * 64, (cc + 1) * 64)
            S.ts(R["Ecs"][:, cs], R["cum"][:, cs], -1.0, R["cum"][:, cc * 64 + 63:cc * 64 + 64], ALU.mult, ALU.add)
        S.act(R["Ecs"][:, :], R["Ecs"][:, :], AF.Exp)
        S.tt(R["BKt"][:, 0, :], R["dif"][0:64, :], R["Ecs"][:, :], ALU.mult)
        S.tt(R["BKt"][:, 1, :], R["kr"][:, :], R["Ecs"][:, :], ALU.mult)
        yt = R["yt"][t % 2]
        def par(cc):
            cs = slice(cc * 64, (cc + 1) * 64)
            k = cc
            idb = R["ident_bf"][0:64, 0:64]
            pst = _psum6(S)
            S.mm(pst[0:64, 0:64], R["vb"][:, cs], idb)
            S.mm(pst[0:64, 64:128], R["abf"][:, cs], idb)
            S.mm(pst[0:64, 128:192], R["BKt"][:, 0, cs], idb)
            S.mm(pst[0:64, 192:256], R["BKt"][:, 1, cs], idb)
            tm = R["tm"][k]
            S.copy(tm[:, :, :], pst[0:64, 0:256].map(lambda a: a.rearrange("p (f k) -> p f k", k=64)), eng="act")
            S.copy(R["atm32"][k][:, :], pst[0:64, 64:128], eng="dve")
            psA = _psum6(S)
            S.mm(psA[0:64, 0:128], R["bb"][:, cs], abr[:, cc, :])
            S.mm(psA[0:64, 128:256], R["kb"][:, cs], abr[:, cc, :])
            NN = R["NN"][k]
            S.tt(NN[:, 0:64], psA[0:64, 0:64], R["m4"][:, 0:64], ALU.mult)
            SCb = R["SCb"][k]
            S.tt(SCb[:, 0:192], psA[0:64, 64:256], R["m4"][:, 64:256], ALU.mult)
            psn = _psum6(S)
            S.mm(psn[0:64, 0:64], NN[:, 0:64], R["ident"][0:64, 0:64])
            S.copy(NN[:, 64:128], psn[0:64, 0:64], eng="act")
            TT, Tm = neumann_inverse(S, R, NN[:, 0:64], NN[:, 64:128], "r", RR=R["RR"][k], PW=R["PW"][k])
            psx = _psum6(S)
            S.mm(psx[0:64, 0:64], SCb[:, 64:128], tm[:, 0, :])
            S.copy(R["x1"][k][:, :], psx[0:64, 0:64], eng="act")
            psU = _psum6(S)
            S.mm(psU[0:64, 0:64], TT, R["x1"][k][:, :])
            S.mm(psU[0:64, 64:128], R["atm32"][k][:, :], TT)
            S.copy(R["u32"][k][:, :], psU[0:64, 0:64], eng="act")
            S.copy(R["wmT"][k][:, :], psU[0:64, 64:128], eng="dve")
        run_interleaved(S, [(lambda c_: (lambda: par(c_)))(c_) for c_ in range(8)], width=3)
        for cc in range(8):
            cs = slice(cc * 64, (cc + 1) * 64)
            k = cc
            tm = R["tm"][k]
            SCb = R["SCb"][k]
            psD = _psum6(S)
            S.mm(psD[0:64, 0:64], R["wmT"][k][:, :], R["Hb"][:, :])
            S.tt(R["Db"][k][:, :], R["u32"][k][:, :], psD[0:64, 0:64], ALU.add)
            psY = _psum6(S)
            S.mm(psY[0:64, 0:64], R["Hb"][:, :], abr[:, cc, 64:128], start=True, stop=False)
            S.mm(psY[0:64, 0:64], R["Db"][k][:, :], SCb[:, 0:64], start=False, stop=False)
            S.mm(psY[0:64, 0:64], tm[:, 0, :], SCb[:, 128:192], start=False, stop=True)
            S.mm(psY[0:64, 64:128], tm[:, 2, :], R["Db"][k][:, :], start=True, stop=False)
            S.mm(psY[0:64, 64:128], tm[:, 3, :], tm[:, 0, :], start=False, stop=True)
            S.stt(R["H32"][:, :], R["H32"][:, :], R["wc8"][:, cc:cc + 1], psY[0:64, 64:128], ALU.mult, ALU.add)
            S.copy(R["Hb"][:, :], R["H32"][:, :], eng="act")
            S.copy(yt[:, cs], psY[0:64, 0:64], eng="act")
        S.copy(R["yb"][:, :], yt[:, :], eng="pool")
        psm = _psum6(S)
        S.mm(psm[0:64, :], R["ones_bf"][0:64, 0:64], R["yb"][:, :])
        S.stt(R["yc"][:, :], psm[0:64, :], -1.0 / 64, yt[:, :], ALU.mult, ALU.add)
        S.act(R["sqb"][:, :], R["yc"][:, :], AF.Square)
        pss = _psum6(S)
        S.mm(pss[0:64, :], R["ones_bf"][0:64, 0:64], R["sqb"][:, :])
        S.act(R["rs"][:, :], pss[0:64, :], AF.Sqrt, bias=R["eps_ln"][:, 0:1], scale=1.0 / 64)
        S.recip(R["rs"][:, :], R["rs"][:, :])
        S.tt(R["yc"][:, :], R["yc"][:, :], R["rs"][:, :], ALU.mult)
        S.ts(R["yc"][:, :], R["yc"][:, :], pc[0:64, 10:11], pc[0:64, 11:12], ALU.mult, ALU.add)
        S.tt(R["pr"][:, :], c["r"][:, :], R["kr"][:, :], ALU.mult)
        psb = _psum6(S)
        S.mm(psb[0:64, :], mats[0:64, 3, :], R["pr"][:, :])
        S.tt(R["bon"][:, :], psb[0:64, :], c["v"][:, :], ALU.mult)
        S.tt(R["yc"][:, :], R["yc"][:, :], R["bon"][:, :], ALU.add)
        yout = R["yout"][t % 2]
        S.tt(yout[:, :], R["yc"][:, :], R["gout"][:, :], ALU.mult)
        S.dma(Y[yrow0:yrow0 + 64, tsl], yout[:, :], final=True)


NTOK_ALL = 16384
SEQ = 8192
GATE_OFF = 6936
NTILES = NTOK_ALL // 512


def common_consts(S, Dc):
    C = {}
    C["ones_f"] = S.sb("c_ones_f", [128, 128], F32); S.memset(C["ones_f"][:, :], 1.0)
    C["ones_bf"] = S.sb("c_ones_bf", [128, 128], BF16); S.copy(C["ones_bf"][:, :], C["ones_f"][:, :], eng="pool")
    C["onecol"] = S.sb("c_onecol", [128, 1], F32); S.memset(C["onecol"][:, :], 1.0)
    C["eps6"] = S.sb("c_eps6", [128, 1], F32); S.memset(C["eps6"][:, :], 1e-6)
    for k in ("ident", "tri", "maskneg", "gm_incl", "gm_ns"):
        C[k] = S.sb("c_" + k, [128, 128], F32); S.dma(C[k][:, :], Dc[k][:, :])
    C["ident_bf"] = S.sb("c_ident_bf", [128, 128], BF16); S.copy(C["ident_bf"][:, :], C["ident"][:, :], eng="pool")
    return C


def host_consts():
    tri, maskneg = ssd_consts()
    incl, ns = gdn_consts()
    gi = np.zeros((128, 128), np.float32); gi[:64, :64] = incl
    gn = np.zeros((128, 128), np.float32); gn[:64, :64] = ns
    return {"ident": np.eye(128, dtype=np.float32), "tri": tri, "maskneg": maskneg, "gm_incl": gi, "gm_ns": gn}


def ssd_cols(h):
    g = h // 4
    return np.concatenate([5904 + h * 64 + np.arange(64), 5392 + h * 64 + np.arange(64), 5904 + 512 + g * 128 + np.arange(128),
                           5904 + 768 + g * 128 + np.arange(128), [6928 + h]])


def ssd_host(i, h, d):
    g = h // 4
    cols = ssd_cols(h)
    wd = np.ascontiguousarray(d["w_in"][i][:, cols].reshape(8, 128, 385).transpose(1, 0, 2))
    cwf = d["mamba_conv_w"][i]; cb = d["mamba_conv_b"][i]
    cw = np.zeros((128, 3, 5), np.float32)
    chans = [h * 64 + np.arange(64), 512 + g * 128 + np.arange(128), 768 + g * 128 + np.arange(128)]
    for gi, ch in enumerate(chans):
        cw[:len(ch), gi, 0:4] = cwf[:, ch].T
        cw[:len(ch), gi, 4] = cb[ch]
    sc = np.zeros((128, 3), np.float32)
    sc[:, 0] = d["mamba_dt_bias"][i][h]; sc[:, 1] = d["mamba_A_log"][i][h]; sc[:, 2] = d["mamba_D"][i][h]
    return {"wd": wd, "cw": cw, "sc": sc}


def gdn_host(i, h, d):
    cols = np.concatenate([1536 + h * 64 + np.arange(64), 2048 + h * 64 + np.arange(64), 2560 + h * 64 + np.arange(64),
                           3072 + h * 64 + np.arange(64), [3584 + h], [3592 + h], [3584 + h] * 64, [3592 + h] * 64]).astype(np.int64)
    wg = np.ascontiguousarray(d["w_in"][i][:, cols].reshape(8, 128, 386).transpose(1, 0, 2))
    cwf = d["gdn_conv_w"][i]
    cw = np.zeros((64, 3, 4), np.float32)
    for gi in range(3):
        cw[:, gi, :] = cwf[:, gi * 512 + h * 64 + np.arange(64)].T
    sc = np.zeros((128, 3), np.float32)
    sc[:, 0] = d["gdn_A_log"][i][h]; sc[:, 1] = d["gdn_dt_bias"][i][h]
    return {"wg": wg, "cw": cw, "sc": sc, "nw": d["gdn_norm_w"][i].reshape(64, 1).copy()}


def moba_host(i, h, d):
    W = d["w_in"][i]
    cols = np.concatenate([np.arange(h * 64, h * 64 + 64) + o for o in (0, 512, 1024)])
    wa = np.ascontiguousarray(W[:, cols].reshape(8, 128, 192).transpose(1, 0, 2))
    biasT, b31 = moba_host_consts(d["rel_bias"], h)
    return {"wa": wa, "qw": d["moba_q_norm"][i].reshape(64, 1).copy(), "kw": d["moba_k_norm"][i].reshape(64, 1).copy(),
            "biasT": biasT, "b31": b31}


def tok_host(i, c, d):
    w_in = d["w_in"][i]
    gcols = np.concatenate([GATE_OFF + n * 1024 + c * 128 + np.arange(128) for n in range(4)])
    wg = np.ascontiguousarray(w_in[:, gcols].reshape(8, 128, 512).transpose(1, 0, 2))
    wb = np.stack([d["w_branch"][i][n][:, c * 128:(c + 1) * 128].reshape(4, 128, 128) for n in range(4)], 0)
    wb = np.ascontiguousarray(wb.reshape(16, 128, 128).transpose(1, 0, 2))
    wo = np.ascontiguousarray(d["w_out"][i][:, c * 128:(c + 1) * 128].reshape(8, 128, 128).transpose(1, 0, 2))
    fin = d["ffn_w_in"][i]
    wfi = np.zeros((1024, 768), np.float32)
    for f in range(3):
        fg = c * 3 + f
        if fg < 22:
            wfi[:, f * 128:(f + 1) * 128] = fin[:, fg * 128:(fg + 1) * 128]
            wfi[:, 384 + f * 128:384 + (f + 1) * 128] = fin[:, 2816 + fg * 128:2816 + (fg + 1) * 128]
    wfi = np.ascontiguousarray(wfi.reshape(8, 128, 768).transpose(1, 0, 2))
    fd = np.zeros((24 * 128, 128), np.float32)
    fd[:2816] = d["ffn_w_down"][i][:, c * 128:(c + 1) * 128]
    wfd = np.ascontiguousarray(fd.reshape(24, 128, 128).transpose(1, 0, 2))
    return {"wg": wg, "wb": wb, "wo": wo, "wfi": wfi, "wfd": wfd,
            "n1": np.ascontiguousarray(d["norm1_w"][i].reshape(8, 128).T), "n2": np.ascontiguousarray(d["norm2_w"][i].reshape(8, 128).T),
            "nwD": np.ascontiguousarray(d["mamba_norm_w"][i].reshape(4, 128).T)}


def host_inputs(c, d):
    inp = {}
    inp.update(host_consts())
    inp["m4"] = rwkv_consts()
    inp["onehot"] = moba_onehot()
    xT = d["x"].reshape(NTOK_ALL, 1024).T
    inp["xcol"] = np.ascontiguousarray(xT[c * 128:(c + 1) * 128, :])
    for i in range(2):
        for pre, hd in (("mo", moba_host(i, c, d)), ("ss", ssd_host(i, c, d)), ("gd", gdn_host(i, c, d)),
                        ("rw", rwkv_host(i, c, d)), ("tk", tok_host(i, c, d))):
            for k, v in hd.items():
                inp[f"L{i}_{pre}_{k}"] = np.ascontiguousarray(v.astype(np.float32))
    return inp


def norm_pass(S, C, src3, Hdst, nw):
    xs = [S.sb(f"np_x{i}", [128, 8, 512], F32) for i in range(2)]
    hs = [S.sb(f"np_h{i}", [128, 8, 512], BF16) for i in range(2)]
    sq = [S.sb(f"np_sq{i}", [128, 512], BF16) for i in range(2)]
    rstd = S.sb("np_rstd", [128, 512], F32)
    for t in range(NTILES):
        tsl = slice(t * 512, (t + 1) * 512)
        x = xs[t % 2]
        h = hs[t % 2]
        S.dma(x[:, :, :], src3[:, :, tsl].map(lambda a: a.rearrange("c p t -> p c t")), q=("sp" if t % 2 == 0 else "pool"))
        ps = _psum6(S)
        for c in range(8):
            s = sq[c % 2]
            S.act(s[:, :], x[:, c, :], AF.Square)
            S.mm(ps[:, :], C["ones_bf"][:, :], s[:, :], start=(c == 0), stop=(c == 7))
        S.act(rstd[:, :], ps[:, :], AF.Sqrt, bias=C["eps6"][:, 0:1], scale=1.0 / 1024)
        S.recip(rstd[:, :], rstd[:, :])
        for c in range(8):
            S.stt(h[:, c, :], x[:, c, :], nw[:, c:c + 1], rstd[:, :], ALU.mult, ALU.mult)
        S.dma(Hdst[:, :, tsl].map(lambda a: a.rearrange("c p t -> p c t")), h[:, :, :])


def build_main():
    nc = bass.Bass("TRN2", target_bir_lowering=False)
    dummy = {k: np.zeros((8192, 1024), np.float32) for k in ()}
    shapes = host_input_shapes()
    IN = {k: nc.dram_tensor(k, list(shp), F32, kind="ExternalInput").ap() for k, shp in shapes.items()}
    OUT = nc.dram_tensor("xout", [128, NTOK_ALL], F32, kind="ExternalOutput").ap()

    def dt_(name, shape, dt):
        return nc.dram_tensor(name, list(shape), dt).ap()
    XC = dt_("XC", [128, NTOK_ALL], F32)
    XG = dt_("XG", [1024, NTOK_ALL], F32)
    H1 = dt_("H1", [8, 128, NTOK_ALL], BF16)
    YS = dt_("YS", [256, NTOK_ALL], BF16)
    YG = dt_("YG", [2048, NTOK_ALL], BF16)
    MI = dt_("MI", [128, NTOK_ALL], BF16)
    MG = dt_("MG", [1024, NTOK_ALL], BF16)
    XMI = dt_("XMI", [128, NTOK_ALL], F32)
    XMG = dt_("XMG", [1024, NTOK_ALL], F32)
    AI = dt_("AI", [384, NTOK_ALL], BF16)
    AG = dt_("AG", [3072, NTOK_ALL], BF16)
    VF = dt_("VF", [64, NTOK_ALL], F32)
    tXC, tXG, tH1, tYS, tYG, tMI, tMG, tXMI, tXMG, tAI, tAG, tVF, tOUT = [T(None, n) for n in
        ("XC", "XG", "H1", "YS", "YG", "MI", "MG", "XMI", "XMG", "AI", "AG", "VF", "OUT")]
    ALL8 = [list(range(8))]
    with ExitStack() as st:
        S = Sched(nc, st)
        _psum_setup(S)
        C = common_consts(S, IN)
        for q in range(4):
            sl = slice(q * 4096, (q + 1) * 4096)
            S.dma(View(tXC, XC)[:, sl], IN["xcol"][:, sl])
        for L in range(2):
            P = lambda pre, k: IN[f"L{L}_{pre}_{k}"]
            S.cc("AllGather", ALU.bypass, ALL8, XC[:, :], XG[:, :], reads=[tXC], writes=[tXG])
            with S.scope():
                n1 = S.sb("n1", [128, 8], F32)
                S.dma(n1[:, :], P("tk", "n1")[:, :])
                norm_pass(S, C, View(tXG, XG.rearrange("(c p) t -> c p t", p=128)), View(tH1, H1), n1)
            vH1 = View(tH1, H1)
            vYS = View(tYS, YS)

            def mk_load_h(b, hbs):
                def load_h(t):
                    hb = hbs[t % 2]
                    c0 = b * SEQ + t * 512
                    S.dma(hb[:, :, :], vH1[:, :, c0:c0 + 512].map(lambda a: a.rearrange("c p t -> p c t")), q=("sp" if t % 2 == 0 else "pool"))
                    return hb
                return load_h
            with S.scope():
                hbs = [S.sb(f"hbA{i}", [128, 8, 512], BF16) for i in range(2)]
                D = {k: P("mo", k) for k in ("wa", "qw", "kw", "biasT", "b31")}
                D["onehot"] = IN["onehot"]; D["ident"] = IN["ident"]
                R = moba_setup(S, D)
                for b in range(2):
                    moba_batch(S, R, mk_load_h(b, hbs), vYS, b * SEQ)
            with S.scope():
                hbs = [S.sb(f"hbB{i}", [128, 8, 512], BF16) for i in range(2)]
                R = gdn_setup(S, {k: P("gd", k) for k in ("wg", "cw", "sc", "nw")}, C)
                for b in range(2):
                    gdn_batch(S, R, mk_load_h(b, hbs), vYS, 64, b * SEQ)
            with S.scope():
                hbs = [S.sb(f"hbC{i}", [128, 8, 512], BF16) for i in range(2)]
                keys = ["wr", "pc", "mats"] + (["muv", "vdown", "vup"] if L > 0 else [])
                D = {k: P("rw", k) for k in keys}
                D["m4"] = IN["m4"]
                R = rwkv_setup(S, D, C, L > 0)
                for b in range(2):
                    rwkv_batch(S, R, mk_load_h(b, hbs), vYS, 128, b * SEQ, View(tVF, VF))
            with S.scope():
                hbs = [S.sb(f"hbD{i}", [128, 8, 512], BF16) for i in range(2)]
                R = ssd_setup(S, {k: P("ss", k) for k in ("wd", "cw", "sc")}, C)
                for b in range(2):
                    ssd_batch(S, R, mk_load_h(b, hbs), vYS, 192, b * SEQ)
            S.cc("AllGather", ALU.bypass, ALL8, YS[:, :], YG[:, :], reads=[tYS], writes=[tYG])
            with S.scope():
                Wg = load_w_bf(S, "tk_wg", P("tk", "wg"), 512)
                Wb = load_w_bf(S, "tk_wb", P("tk", "wb"), 128, kc=16)
                nwD = S.sb("tk_nwD", [128, 4], F32)
                S.dma(nwD[:, :], P("tk", "nwD")[:, :])
                hbs = [S.sb(f"t1_h{i}", [128, 8, 512], BF16) for i in range(2)]
                ybs = [S.sb(f"t1_y{i}", [128, 16, 512], BF16) for i in range(2)]
                gts = [S.sb(f"t1_g{i}", [128, 512], BF16) for i in range(2)]
                sqd = [S.sb(f"t1_sq{i}", [128, 512], BF16) for i in range(2)]
                rsd = S.sb("t1_rs", [128, 512], F32)
                m32 = S.sb("t1_m32", [128, 512], F32)
                tmp = [S.sb(f"t1_tmp{i}", [128, 512], F32) for i in range(2)]
                mo = [S.sb(f"t1_mo{i}", [128, 512], BF16) for i in range(2)]
                vYG = View(tYG, YG)
                vMI = View(tMI, MI)
                gi_ = 0
                for t in range(NTILES):
                    tsl = slice(t * 512, (t + 1) * 512)
                    hb = hbs[t % 2]
                    yb = ybs[t % 2]
                    S.dma(hb[:, :, :], vH1[:, :, tsl].map(lambda a: a.rearrange("c p t -> p c t")))
                    for n in range(4):
                        for two in range(2):
                            src = vYG[:, tsl].map(lambda a: a.rearrange("(j two n d) t -> two n j d t", two=2, n=4, d=64)[two, n, :, :, :].rearrange("j d t -> d j t"))
                            S.dma(yb[two * 64:(two + 1) * 64, n * 4:(n + 1) * 4, :], src, q=("pool" if two else "sp"))
                    for g in range(2):
                        ps = _psum6(S)
                        for jj in range(2):
                            s = sqd[jj]
                            S.act(s[:, :], yb[:, 12 + 2 * g + jj, :], AF.Square)
                            S.mm(ps[:, :], C["ones_bf"][:, :], s[:, :], start=(jj == 0), stop=(jj == 1))
                        S.act(rsd[:, :], ps[:, :], AF.Sqrt, bias=C["eps6"][:, 0:1], scale=1.0 / 256)
                        S.recip(rsd[:, :], rsd[:, :])
                        for jj in range(2):
                            j = 2 * g + jj
                            S.stt(yb[:, 12 + j, :], yb[:, 12 + j, :], nwD[:, j:j + 1], rsd[:, :], ALU.mult, ALU.mult)
                    for n in range(4):
                        gt = gts[gi_ % 2]
                        gi_ += 1
                        ps = _psum6(S)
                        proj_fm(S, ps[:, :], Wg, n * 128, 128, hb)
                        S.act(gt[:, :], ps[:, :], AF.Sigmoid)
                        ps2 = _psum6(S)
                        for j in range(4):
                            S.mm(ps2[:, :], Wb[:, n * 4 + j, :], yb[:, n * 4 + j, :], start=(j == 0), stop=(j == 3))
                        if n == 0:
                            S.tt(m32[:, :], ps2[:, :], gt[:, :], ALU.mult)
                        else:
                            tm_ = tmp[n % 2]
                            S.tt(tm_[:, :], ps2[:, :], gt[:, :], ALU.mult)
                            S.tt(m32[:, :], m32[:, :], tm_[:, :], ALU.add, eng="pool")
                    S.copy(mo[t % 2][:, :], m32[:, :], eng="act")
                    S.dma(vMI[:, tsl], mo[t % 2][:, :])
            S.cc("AllGather", ALU.bypass, ALL8, MI[:, :], MG[:, :], reads=[tMI], writes=[tMG])
            with S.scope():
                Wo = load_w_bf(S, "tk_wo", P("tk", "wo"), 128)
                mbs = [S.sb(f"t2_m{i}", [128, 8, 512], BF16) for i in range(2)]
                xts = [S.sb(f"t2_x{i}", [128, 512], F32) for i in range(2)]
                xos = [S.sb(f"t2_xo{i}", [128, 512], F32) for i in range(2)]
                vMG = View(tMG, MG); vXC = View(tXC, XC); vXMI = View(tXMI, XMI)
                for t in range(NTILES):
                    tsl = slice(t * 512, (t + 1) * 512)
                    mb = mbs[t % 2]
                    S.dma(mb[:, :, :], vMG[:, tsl].map(lambda a: a.rearrange("(c p) t -> p c t", p=128)))
                    S.dma(xts[t % 2][:, :], vXC[:, tsl], q="pool")
                    ps = _psum6(S)
                    proj_fm(S, ps[:, :], Wo, 0, 128, mb)
                    S.tt(xos[t % 2][:, :], xts[t % 2][:, :], ps[:, :], ALU.add)
                    S.dma(vXMI[:, tsl], xos[t % 2][:, :])
            S.cc("AllGather", ALU.bypass, ALL8, XMI[:, :], XMG[:, :], reads=[tXMI], writes=[tXMG])
            with S.scope():
                n2 = S.sb("n2", [128, 8], F32)
                S.dma(n2[:, :], P("tk", "n2")[:, :])
                norm_pass(S, C, View(tXMG, XMG.rearrange("(c p) t -> c p t", p=128)), View(tH1, H1), n2)
            with S.scope():
                Wfi = load_w_bf(S, "tk_wfi", P("tk", "wfi"), 768)
                hbs = [S.sb(f"t3_h{i}", [128, 8, 512], BF16) for i in range(2)]
                sgs = [S.sb(f"t3_sg{i}", [128, 512], F32) for i in range(2)]
                abs_ = [S.sb(f"t3_a{i}", [128, 3, 512], BF16) for i in range(2)]
                vAI = View(tAI, AI)
                k_ = 0
                for t in range(NTILES):
                    tsl = slice(t * 512, (t + 1) * 512)
                    hb = hbs[t % 2]
                    S.dma(hb[:, :, :], vH1[:, :, tsl].map(lambda a: a.rearrange("c p t -> p c t")), q=("sp" if t % 2 == 0 else "pool"))
                    ab = abs_[t % 2]
                    for f in range(3):
                        psg = _psum6(S)
                        proj_fm(S, psg[:, :], Wfi, f * 128, 128, hb)
                        sg = sgs[k_ % 2]
                        k_ += 1
                        S.act(sg[:, :], psg[:, :], AF.Silu)
                        psu = _psum6(S)
                        proj_fm(S, psu[:, :], Wfi, 384 + f * 128, 128, hb)
                        S.tt(ab[:, f, :], psu[:, :], sg[:, :], ALU.mult)
                    S.dma(vAI[:, tsl].map(lambda a: a.rearrange("(f p) t -> p f t", p=128)), ab[:, :, :])
            S.cc("AllGather", ALU.bypass, ALL8, AI[:, :], AG[:, :], reads=[tAI], writes=[tAG])
            with S.scope():
                Wd = load_w_bf(S, "tk_wfd", P("tk", "wfd"), 128, kc=24)
                abs_ = [S.sb(f"t4_a{i}", [128, 24, 512], BF16) for i in range(2)]
                xts = [S.sb(f"t4_x{i}", [128, 512], F32) for i in range(2)]
                xos = [S.sb(f"t4_xo{i}", [128, 512], F32) for i in range(2)]
                vAG = View(tAG, AG); vXMI = View(tXMI, XMI)
                for t in range(NTILES):
                    tsl = slice(t * 512, (t + 1) * 512)
                    ab = abs_[t % 2]
                    S.dma(ab[:, 0:12, :], vAG[0:1536, tsl].map(lambda a: a.rearrange("(c p) t -> p c t", p=128)))
                    S.dma(ab[:, 12:24, :], vAG[1536:3072, tsl].map(lambda a: a.rearrange("(c p) t -> p c t", p=128)), q="pool")
                    S.dma(xts[t % 2][:, :], vXMI[:, tsl], q="pool")
                    ps = _psum6(S)
                    proj_fm(S, ps[:, :], Wd, 0, 128, ab, kc=24)
                    S.tt(xos[t % 2][:, :], xts[t % 2][:, :], ps[:, :], ALU.add)
                    if L == 0:
                        S.dma(View(tXC, XC)[:, tsl], xos[t % 2][:, :])
                    else:
                        S.dma(View(tOUT, OUT)[:, tsl], xos[t % 2][:, :], final=True)
        S.emit()
    return nc, S


_SHAPES = None


def host_input_shapes():
    return _SHAPES


def kernel(**inputs):
    global _SHAPES
    d = {k: np.asarray(v) for k, v in inputs.items()}
    in_maps = [host_inputs(c, d) for c in range(8)]
    _SHAPES = {k: v.shape for k, v in in_maps[0].items()}
    nc, S = build_main()
    res = run_bass_kernel_spmd(nc, in_maps, core_ids=list(range(8)))
    outT = np.concatenate([res.results[c]["xout"] for c in range(8)], axis=0)
    return np.ascontiguousarray(outT.T).reshape(2, SEQ, 1024).astype(np.float32)
```
